# Optimizing a Trainium2 kernel written in Bass

```python
import math
import jax, jax.numpy as jnp
from jax import lax
import numpy as np

D_MODEL = 1024
BATCH = 1
SEQ = 16384
DEPTH = 4

HEAD_DIM = 64
N_HEADS_A = D_MODEL // (2 * HEAD_DIM)
N_HEADS_B = D_MODEL // (2 * HEAD_DIM)
N_HEADS_C = D_MODEL // (2 * HEAD_DIM)
A_WIDTH = N_HEADS_A * HEAD_DIM
B_WIDTH = N_HEADS_B * HEAD_DIM
HYB_IN = 3 * A_WIDTH + 3 * B_WIDTH + N_HEADS_B
DIFF_QK = 2 * N_HEADS_C * HEAD_DIM
DIFF_V = N_HEADS_C * 2 * HEAD_DIM
D_FF = 2816
CONV_WIDTH = 3
ROPE_THETA = 10000.0
DILATED_PATTERNS = ((128, 1), (512, 4), (2048, 16))
BLOCK_Q = 128
NORM_EPS = 1e-6
SUBLN_EPS = 1e-5
N_EVEN = (DEPTH + 1) // 2
N_ODD = DEPTH // 2

kernel_name = "hybrid_dilated_fox_diffattn_convglu"


def rms_norm(x, g, eps=NORM_EPS):
    xf = x.astype(jnp.float32)
    y = xf * lax.rsqrt(jnp.mean(xf * xf, axis=-1, keepdims=True) + eps)
    return (y * g.astype(jnp.float32)).astype(x.dtype)


def rope_tables(seq):
    inv = 1.0 / (ROPE_THETA ** (jnp.arange(0, HEAD_DIM, 2, dtype=jnp.float32) / HEAD_DIM))
    ang = jnp.arange(seq, dtype=jnp.float32)[:, None] * inv[None, :]
    ang = jnp.concatenate([ang, ang], axis=-1)
    return jnp.cos(ang)[None, :, None, :], jnp.sin(ang)[None, :, None, :]


def apply_rope(x, cos, sin):
    x1, x2 = jnp.split(x, 2, axis=-1)
    rot = jnp.concatenate([-x2, x1], axis=-1)
    return (x.astype(jnp.float32) * cos + rot.astype(jnp.float32) * sin).astype(x.dtype)


def local_window_attention(q, k, v, span):
    n, L, h, dh = q.shape
    nb = -(-L // BLOCK_Q)
    lp = nb * BLOCK_Q
    pad = lp - L
    padt = lambda t: jnp.pad(t, ((0, 0), (0, pad), (0, 0), (0, 0)))
    q, k, v = padt(q), padt(k), padt(v)

    def blocks_with_prev(t):
        te = jnp.pad(t, ((0, 0), (BLOCK_Q, 0), (0, 0), (0, 0)))
        prev = te[:, :lp].reshape(n, nb, BLOCK_Q, h, t.shape[-1])
        cur = te[:, BLOCK_Q:].reshape(n, nb, BLOCK_Q, h, t.shape[-1])
        return jnp.concatenate([prev, cur], axis=2)

    kb, vb = blocks_with_prev(k), blocks_with_prev(v)
    qb = q.reshape(n, nb, BLOCK_Q, h, dh)
    s = jnp.einsum('nbqhd,nbkhd->nbhqk', qb, kb, preferred_element_type=jnp.float32)
    i = jnp.arange(BLOCK_Q)[:, None]
    j = jnp.arange(2 * BLOCK_Q)[None, :]
    delta = i - j + BLOCK_Q
    band = (delta >= 0) & (delta <= span)
    blk = jnp.arange(nb)[:, None, None]
    valid = band[None] & (blk * BLOCK_Q - BLOCK_Q + j[None] >= 0)
    s = jnp.where(valid[None, :, None], s, -jnp.inf)
    m = jnp.max(s, axis=-1, keepdims=True)
    p = jnp.exp(s - m)
    denom = jnp.sum(p, axis=-1, keepdims=True)
    o = jnp.einsum('nbhqk,nbkhd->nbqhd', (p / denom).astype(v.dtype), vb)
    lse = (m + jnp.log(denom))[..., 0]
    lse = jnp.transpose(lse, (0, 1, 3, 2)).reshape(n, lp, h)[:, :L]
    o = o.reshape(n, lp, h, dh)[:, :L]
    return o, lse


def dilated_attention(q, k, v):
    b, s, h, dh = q.shape
    outs, lses = [], []
    for window, dil in DILATED_PATTERNS:
        L = s // dil

        def to_sub(t):
            t = t.reshape(b, L, dil, *t.shape[2:])
            return jnp.swapaxes(t, 1, 2).reshape(b * dil, L, *t.shape[3:])

        def from_sub(t):
            t = t.reshape(b, dil, L, *t.shape[2:])
            return jnp.swapaxes(t, 1, 2).reshape(b, s, *t.shape[3:])

        o, lse = local_window_attention(to_sub(q), to_sub(k), to_sub(v), window // dil)
        outs.append(from_sub(o))
        lses.append(from_sub(lse))
    alpha = jax.nn.softmax(jnp.stack(lses, axis=0), axis=0)
    out = jnp.sum(alpha[..., None] * jnp.stack(outs, axis=0).astype(jnp.float32), axis=0)
    return out.astype(q.dtype)


def to_blocks(t):
    b, s = t.shape[:2]
    return jnp.moveaxis(t.reshape(b, s // BLOCK_Q, BLOCK_Q, *t.shape[2:]), 1, 0)


def from_blocks(t):
    nb, b, bq = t.shape[:3]
    return jnp.moveaxis(t, 0, 1).reshape(b, nb * bq, *t.shape[3:])


def causal_block_probs(qb, k, q_start, bias=None):
    s = jnp.einsum('bqhd,bkhd->bhqk', qb, k, preferred_element_type=jnp.float32)
    if bias is not None:
        s = s + bias
    qpos = q_start + jnp.arange(qb.shape[1])
    kpos = jnp.arange(k.shape[1])
    s = jnp.where(kpos[None, :] <= qpos[:, None], s, -jnp.inf)
    return jax.nn.softmax(s, axis=-1)


def forgetting_attention(q, k, v, logf):
    c = jnp.cumsum(logf, axis=1)
    ck = jnp.transpose(c, (0, 2, 1))
    nb = q.shape[1] // BLOCK_Q

    def step(args):
        start, qb, cb = args
        bias = jnp.transpose(cb, (0, 2, 1))[..., :, None] - ck[:, :, None, :]
        p = causal_block_probs(qb, k, start, bias)
        return jnp.einsum('bhqk,bkhd->bqhd', p.astype(v.dtype), v)

    out = lax.map(step, (jnp.arange(nb) * BLOCK_Q, to_blocks(q), to_blocks(c)))
    return from_blocks(out)


def differential_attention(q1, q2, k1, k2, v, lam):
    nb = q1.shape[1] // BLOCK_Q

    def step(args):
        start, q1b, q2b = args
        a = causal_block_probs(q1b, k1, start) - lam * causal_block_probs(q2b, k2, start)
        return jnp.einsum('bhqk,bkhd->bqhd', a.astype(v.dtype), v)

    out = lax.map(step, (jnp.arange(nb) * BLOCK_Q, to_blocks(q1), to_blocks(q2)))
    return from_blocks(out)


def hybrid_mixer(n, w_in, b_f, w_out, cos, sin):
    b, s, _ = n.shape
    proj = n @ w_in
    cuts = [A_WIDTH, 2 * A_WIDTH, 3 * A_WIDTH,
            3 * A_WIDTH + B_WIDTH, 3 * A_WIDTH + 2 * B_WIDTH, 3 * A_WIDTH + 3 * B_WIDTH]
    qa, ka, va, qb, kb, vb, fb = jnp.split(proj, cuts, axis=-1)
    heads = lambda t: t.reshape(b, s, -1, HEAD_DIM)
    scale = HEAD_DIM ** -0.5
    oa = dilated_attention(apply_rope(heads(qa), cos, sin) * scale,
                           apply_rope(heads(ka), cos, sin), heads(va))
    logf = jax.nn.log_sigmoid((fb + b_f).astype(jnp.float32))
    ob = forgetting_attention(heads(qb) * scale, heads(kb), heads(vb), logf)
    o = jnp.concatenate([oa.reshape(b, s, A_WIDTH), ob.reshape(b, s, B_WIDTH)], axis=-1)
    return o @ w_out


def diff_mixer(n, w_qkv, lam_params, subln_g, w_out, cos, sin, layer_idx):
    b, s, _ = n.shape
    q, k, v = jnp.split(n @ w_qkv, [DIFF_QK, 2 * DIFF_QK], axis=-1)
    scale = HEAD_DIM ** -0.5
    q = (apply_rope(q.reshape(b, s, 2 * N_HEADS_C, HEAD_DIM), cos, sin) * scale).reshape(b, s, N_HEADS_C, 2, HEAD_DIM)
    k = apply_rope(k.reshape(b, s, 2 * N_HEADS_C, HEAD_DIM), cos, sin).reshape(b, s, N_HEADS_C, 2, HEAD_DIM)
    v = v.reshape(b, s, N_HEADS_C, 2 * HEAD_DIM)
    lam_init = 0.8 - 0.6 * math.exp(-0.3 * layer_idx)
    lp = lam_params.astype(jnp.float32)
    lam = jnp.exp(jnp.sum(lp[0] * lp[1])) - jnp.exp(jnp.sum(lp[2] * lp[3])) + lam_init
    o = differential_attention(q[:, :, :, 0], q[:, :, :, 1], k[:, :, :, 0], k[:, :, :, 1], v, lam)
    o = rms_norm(o, subln_g, SUBLN_EPS) * (1.0 - lam_init)
    return o.reshape(b, s, DIFF_V) @ w_out


def conv_glu_ffn(n, w_up, conv_w, conv_b, w_down):
    s = n.shape[1]
    gate, up = jnp.split(n @ w_up, 2, axis=-1)
    gp = jnp.pad(gate, ((0, 0), (CONV_WIDTH - 1, 0), (0, 0)))
    conv = conv_b
    for j in range(CONV_WIDTH):
        conv = conv + gp[:, j:j + s] * conv_w[j]
    return (jax.nn.silu(conv) * up) @ w_down


def setup_inputs(seed: int = 0) -> dict:
    key = jax.random.key(seed)
    ks = jax.random.split(key, 20)
    nrm = lambda k, shape, fan_in: jax.random.normal(k, shape, jnp.float32) * fan_in ** -0.5
    return {
        "x": jax.random.normal(ks[0], (BATCH, SEQ, D_MODEL), jnp.float32),
        "attn_norm": 1.0 + 0.02 * jax.random.normal(ks[1], (DEPTH, D_MODEL), jnp.float32),
        "ffn_norm": 1.0 + 0.02 * jax.random.normal(ks[2], (DEPTH, D_MODEL), jnp.float32),
        "final_norm": 1.0 + 0.02 * jax.random.normal(ks[3], (D_MODEL,), jnp.float32),
        "hyb_w_in": nrm(ks[4], (N_EVEN, D_MODEL, HYB_IN), D_MODEL),
        "hyb_b_f": 2.0 + 0.5 * jax.random.normal(ks[5], (N_EVEN, N_HEADS_B), jnp.float32),
        "hyb_w_out": nrm(ks[6], (N_EVEN, A_WIDTH + B_WIDTH, D_MODEL), A_WIDTH + B_WIDTH),
        "diff_w_qkv": nrm(ks[7], (N_ODD, D_MODEL, 2 * DIFF_QK + DIFF_V), D_MODEL),
        "diff_lambda": 0.1 * jax.random.normal(ks[8], (N_ODD, 4, HEAD_DIM), jnp.float32),
        "diff_subln": 1.0 + 0.02 * jax.random.normal(ks[9], (N_ODD, 2 * HEAD_DIM), jnp.float32),
        "diff_w_out": nrm(ks[10], (N_ODD, DIFF_V, D_MODEL), DIFF_V),
        "ffn_w_up": nrm(ks[11], (DEPTH, D_MODEL, 2 * D_FF), D_MODEL),
        "ffn_conv_w": nrm(ks[12], (DEPTH, CONV_WIDTH, D_FF), CONV_WIDTH),
        "ffn_conv_b": 0.01 * jax.random.normal(ks[13], (DEPTH, D_FF), jnp.float32),
        "ffn_w_down": nrm(ks[14], (DEPTH, D_FF, D_MODEL), D_FF),
    }


def reference(x, attn_norm, ffn_norm, final_norm, hyb_w_in, hyb_b_f, hyb_w_out,
              diff_w_qkv, diff_lambda, diff_subln, diff_w_out,
              ffn_w_up, ffn_conv_w, ffn_conv_b, ffn_w_down):
    cos, sin = rope_tables(x.shape[1])
    h = x
    for l in range(DEPTH):
        n = rms_norm(h, attn_norm[l])
        if l % 2 == 0:
            e = l // 2
            h = h + hybrid_mixer(n, hyb_w_in[e], hyb_b_f[e], hyb_w_out[e], cos, sin)
        else:
            o = l // 2
            h = h + diff_mixer(n, diff_w_qkv[o], diff_lambda[o], diff_subln[o], diff_w_out[o], cos, sin, l)
        h = h + conv_glu_ffn(rms_norm(h, ffn_norm[l]), ffn_w_up[l], ffn_conv_w[l], ffn_conv_b[l], ffn_w_down[l])
    return rms_norm(h, final_norm)
```

```python
import numpy as np, contextlib
import ml_dtypes
import concourse.bass as bass
import concourse.mybir as mybir
from concourse.bass_utils import run_bass_kernel_spmd
F32 = mybir.dt.float32; BF16 = mybir.dt.bfloat16
AF = mybir.ActivationFunctionType; ALU = mybir.AluOpType
NPBF = ml_dtypes.bfloat16


ENGINES = ("pe", "act", "dve", "pool", "sp")
STRICT_SAME_ENGINE = ("act", "dve", "pool")


class Op:
    __slots__ = ("eng", "fn", "deps", "is_dma", "sem", "val", "src", "idx")

    def __init__(self, eng, fn, is_dma):
        self.eng = eng; self.fn = fn; self.deps = []; self.is_dma = is_dma
        self.sem = None; self.val = None; self.src = False; self.idx = None


class Sched:
    def __init__(self, nc, strict=True):
        self.nc = nc
        self.ops = {e: [] for e in ENGINES}
        self.last_w = {}
        self.readers = {}
        self.strict = strict
        self.dma_sems = {}
        self.sem_handles = {}
        self.nsem = 0

    def _add(self, op, reads, writes):
        deps = []
        for r in reads:
            w = self.last_w.get(r)
            if w is not None:
                deps.append(w)
        for r in writes:
            w = self.last_w.get(r)
            if w is not None:
                deps.append(w)
            deps.extend(self.readers.get(r, ()))
        for d in deps:
            if d is op:
                continue
            if (not d.is_dma) and d.eng == op.eng and not op.is_dma:
                if not (self.strict and op.eng in STRICT_SAME_ENGINE):
                    continue
            if d not in op.deps:
                op.deps.append(d)
                d.src = True
        for r in reads:
            self.readers.setdefault(r, []).append(op)
        for r in writes:
            self.last_w[r] = op
            self.readers[r] = []
        op.idx = len(self.ops[op.eng])
        self.ops[op.eng].append(op)
        return op

    def op(self, eng, fn, reads=(), writes=()):
        return self._add(Op(eng, fn, False), reads, writes)

    def dma(self, eng, fn, reads=(), writes=(), key=None):
        o = Op(eng, fn, True)
        key = key if key is not None else ("dma", eng)
        cnt = self.dma_sems.setdefault(key, [0])
        cnt[0] += 16
        o.sem = key; o.val = cnt[0]
        return self._add(o, reads, writes)

    def emit(self, final_waits=()):
        nc = self.nc
        for e in ENGINES:
            c = 0
            for o in self.ops[e]:
                if not o.is_dma and o.src:
                    c += 1
                    o.sem = ("eng", e); o.val = c
        keys = set()
        for e in ENGINES:
            for o in self.ops[e]:
                if o.sem is not None and (o.is_dma or o.src):
                    keys.add(o.sem)
        keys = sorted(keys, key=str)
        import contextlib
        with contextlib.ExitStack() as st:
            for i, k in enumerate(keys):
                self.sem_handles[k] = st.enter_context(nc.semaphore("s%d" % i))
            block = st.enter_context(nc.Block())
            engmap = {"pe": block.tensor, "act": block.scalar, "dve": block.vector,
                      "pool": block.gpsimd, "sp": block.sync}
            for e in ENGINES:
                ops = self.ops[e]
                if not ops and not (e == "sp" and final_waits):
                    continue

                def body(eng, ops=ops, e=e):
                    waited = {}
                    for o in ops:
                        need = {}
                        for d in o.deps:
                            if need.get(d.sem, 0) < d.val:
                                need[d.sem] = d.val
                        for k, v in need.items():
                            if waited.get(k, 0) >= v:
                                continue
                            eng.wait_ge(self.sem_handles[k], v)
                            waited[k] = v
                        ins = o.fn(eng)
                        if o.is_dma:
                            ins.then_inc(self.sem_handles[o.sem], 16)
                        elif o.src:
                            ins.then_inc(self.sem_handles[o.sem], 1)
                    if e == "sp":
                        need = {}
                        for d in final_waits:
                            if need.get(d.sem, 0) < d.val:
                                need[d.sem] = d.val
                        for k, v in need.items():
                            eng.wait_ge(self.sem_handles[k], v)
                engmap[e](body)


NFF = 22
STRICT = True
FF_GROUPS = [list(range(0, 6)), list(range(6, 12)), list(range(12, 17)), list(range(17, 22))]


def build_dense(mode, H=2, NT=2048):
    T = H + NT
    tiles = ([(0, H)] if H else []) + [(H + 512 * i, 512) for i in range(NT // 512)]
    nc = bass.Bass("TRN2", target_bir_lowering=False)
    dt_in = lambda n, s, d=F32: nc.dram_tensor(n, s, d, kind="ExternalInput").ap()
    dt_out = lambda n, s, d=F32: nc.dram_tensor(n, s, d, kind="ExternalOutput").ap()
    hT_d = dt_in("hT", [128, 8, T])
    g_next_d = dt_in("g_next", [128, 8])
    mix = mode in ("full", "final")
    if mix:
        oT_d = dt_in("oT", [128, 8, T], BF16)
        w_out_d = dt_in("w_out", [128, 8, 1024])
        g_ffn_d = dt_in("g_ffn", [128, 8])
        w_up_d = dt_in("w_up", [NFF, 128, 8, 256])
        cw_d = dt_in("cw", [128, NFF, 3])
        cb_d = dt_in("cb", [128, NFF])
        w_down_d = dt_in("w_down", [128, NFF, 1024])
    if mode == "final":
        y_d = dt_out("yT", [128, 8, NT])
    else:
        hT_o = dt_out("hT_out", [128, 8, NT]) if mix else None
        nT_o = dt_out("nT_out", [128, 8, NT], BF16)

    S = Sched(nc, strict=STRICT)
    with contextlib.ExitStack() as st:
        sb = lambda n, s, d=F32: st.enter_context(nc.sbuf_tensor(n, s, d))
        hT = sb("hT_sb", [128, 8, T])
        actT = sb("actT", [128, 8, T], BF16)
        g_next = sb("g_next_sb", [128, 8])
        ones_bf = sb("ones_bf", [128, 128], BF16)
        sqb = sb("sqb", [128, 2, 512], BF16)
        rs = sb("rs", [128, 512])
        ps = [st.enter_context(nc.psum_tensor("ps%d" % i, [128, 512], F32)) for i in range(8)]
        if mix:
            w_out = sb("w_out_sb", [128, 8, 1024], BF16)
            g_ffn = sb("g_ffn_sb", [128, 8])
            cw = sb("cw_sb", [128, NFF, 3])
            cb = sb("cb_sb", [128, NFF])
            GM = max(len(g) for g in FF_GROUPS)
            mT = sb("mT", [128, GM, T], BF16)
            wd = sb("wd", [128, GM, 1024], BF16)
            wu = [sb("wu%d" % i, [128, 8, 256], BF16) for i in range(2)]
            G = [sb("G%d" % i, [128, T + 2]) for i in range(2)]
            c1 = [sb("c1_%d" % i, [128, 512]) for i in range(2)]
            sg = [sb("sg_%d" % i, [128, 512]) for i in range(2)]

        S.op("pool", lambda e: e.memset(ones_bf[:], 1.0), writes=["ones"])
        for c in range(8):
            S.dma("sp", lambda e, c=c: e.dma_start(out=hT[:, c, :], in_=hT_d[:, c, :]),
                  writes=[("h", c, ti) for ti in range(len(tiles))], key=("ldh", c))
        S.dma("sp", lambda e: e.dma_start(out=g_next[:], in_=g_next_d), writes=["g_next"], key="ldg")
        if mix:
            for c in range(8):
                S.dma("act", lambda e, c=c: e.dma_start(out=actT[:, c, :], in_=oT_d[:, c, :]),
                      writes=[("act", c, ti) for ti in range(len(tiles))], key=("ldo", c))
            S.dma("sp", lambda e: e.dma_start(out=g_ffn[:], in_=g_ffn_d), writes=["g_ffn"], key="ldg2")
            S.dma("sp", lambda e: e.dma_start(out=cw[:], in_=cw_d), writes=["cw"], key="ldcw")
            S.dma("sp", lambda e: e.dma_start(out=cb[:], in_=cb_d), writes=["cb"], key="ldcb")
            for c in range(8):
                S.dma("pool", lambda e, c=c: e.dma_start(out=w_out[:, c, :], in_=w_out_d[:, c, :]),
                      writes=[("w_out", c)], key=("ldwo", c))
            for i in range(2):
                S.op("pool", lambda e, i=i: e.memset(G[i][:, 0:2], 0.0), writes=[("Gpre", i)])

        cnt = {"yp": 0, "sq": 0, "gp": 0, "c1": 0}

        def norm_tile(ti, g_sb, gkey, dst, dstkey):
            s, w = tiles[ti]
            ssp = ps[2]
            for c in range(8):
                q = cnt["sq"] % 2; cnt["sq"] += 1
                S.op("act", lambda e, c=c, q=q: e.activation(out=sqb[:, q, :w], in_=hT[:, c, s:s + w], func=AF.Square),
                     reads=[("h", c, ti)], writes=[("sq", q)])
                S.op("pe", lambda e, c=c, q=q: e.matmul(ssp[:, :w], lhsT=ones_bf[:], rhs=sqb[:, q, :w],
                                                        start=(c == 0), stop=(c == 7)),
                     reads=["ones", ("sq", q)], writes=[("ps", 2)])
            S.op("dve", lambda e: e.tensor_scalar(out=rs[:, :w], in0=ssp[:, :w], scalar1=1.0 / 1024, scalar2=1e-6,
                                                  op0=ALU.mult, op1=ALU.add), reads=[("ps", 2)], writes=["rs"])
            S.op("act", lambda e: e.activation(out=rs[:, :w], in_=rs[:, :w], func=AF.Sqrt), reads=["rs"], writes=["rs"])
            S.op("dve", lambda e: e.reciprocal(out=rs[:, :w], in_=rs[:, :w]), reads=["rs"], writes=["rs"])
            for c in range(8):
                S.op("dve", lambda e, c=c: e.scalar_tensor_tensor(out=dst[:, c, s:s + w], in0=hT[:, c, s:s + w],
                                                                  scalar=g_sb[:, c:c + 1], in1=rs[:, :w],
                                                                  op0=ALU.mult, op1=ALU.mult),
                     reads=[("h", c, ti), gkey, "rs"], writes=[(dstkey, c, ti)])

        if mix:
            def p1_tile(ti, s, w):
                for dm in range(8):
                    k = cnt["yp"] % 2; cnt["yp"] += 1
                    for kc in range(8):
                        S.op("pe", lambda e, dm=dm, kc=kc, k=k: e.matmul(ps[k][:, :w], lhsT=w_out[:, kc, dm * 128:(dm + 1) * 128],
                                                                        rhs=actT[:, kc, s:s + w], start=(kc == 0), stop=(kc == 7)),
                             reads=[("w_out", kc), ("act", kc, ti)], writes=[("ps", k)])
                    S.op("dve", lambda e, dm=dm, k=k: e.tensor_tensor(out=hT[:, dm, s:s + w], in0=hT[:, dm, s:s + w],
                                                                      in1=ps[k][:, :w], op=ALU.add),
                         reads=[("h", dm, ti), ("ps", k)], writes=[("h", dm, ti)])
                norm_tile(ti, g_ffn, "g_ffn", actT, "act")
            for ti, (s, w) in enumerate(tiles):
                p1_tile(ti, s, w)
            jc = 0
            for grp in FF_GROUPS:
                S.dma("pool", lambda e, grp=grp: e.dma_start(out=wd[:, 0:len(grp), :], in_=w_down_d[:, grp[0]:grp[0] + len(grp), :]),
                      writes=["wd"], key="ldwd")
                for jj, j in enumerate(grp):
                    sl = jc % 2; jc += 1
                    S.dma("pool", lambda e, j=j, sl=sl: e.dma_start(out=wu[sl][:], in_=w_up_d[j]), writes=[("wu", sl)], key=("ldwu", sl))
                    def ffn_tile(ti, s, w, j, jj, sl):
                        gq = cnt["gp"] % 2; cnt["gp"] += 1
                        gp, upp = ps[3 + gq], ps[5 + gq]
                        for kc in range(8):
                            S.op("pe", lambda e, kc=kc, sl=sl, gp=gp: e.matmul(gp[:, :w], lhsT=wu[sl][:, kc, 0:128], rhs=actT[:, kc, s:s + w],
                                                                            start=(kc == 0), stop=(kc == 7)),
                                 reads=[("wu", sl), ("act", kc, ti)], writes=[("ps", 3 + gq)])
                        for kc in range(8):
                            S.op("pe", lambda e, kc=kc, sl=sl, upp=upp: e.matmul(upp[:, :w], lhsT=wu[sl][:, kc, 128:256], rhs=actT[:, kc, s:s + w],
                                                                              start=(kc == 0), stop=(kc == 7)),
                                 reads=[("wu", sl), ("act", kc, ti)], writes=[("ps", 5 + gq)])
                        S.op("act", lambda e, sl=sl, gp=gp: e.activation(out=G[sl][:, 2 + s:2 + s + w], in_=gp[:, :w], func=AF.Copy),
                             reads=[("ps", 3 + gq)], writes=[("G", sl, ti)])
                        x = cnt["c1"] % 2; cnt["c1"] += 1
                        prev = [("G", sl, ti - 1)] if ti > 0 else [("Gpre", sl)]
                        S.op("dve", lambda e, sl=sl, j=j, x=x: e.tensor_scalar(out=c1[x][:, :w], in0=G[sl][:, 2 + s:2 + s + w],
                                                                              scalar1=cw[:, j, 2:3], scalar2=cb[:, j:j + 1],
                                                                              op0=ALU.mult, op1=ALU.add),
                             reads=[("G", sl, ti), "cw", "cb"], writes=[("c1", x)])
                        S.op("dve", lambda e, sl=sl, j=j, x=x: e.scalar_tensor_tensor(out=c1[x][:, :w], in0=G[sl][:, 1 + s:1 + s + w],
                                                                                     scalar=cw[:, j, 1:2], in1=c1[x][:, :w],
                                                                                     op0=ALU.mult, op1=ALU.add),
                             reads=[("G", sl, ti), "cw"] + prev, writes=[("c1", x)])
                        S.op("dve", lambda e, sl=sl, j=j, x=x: e.scalar_tensor_tensor(out=c1[x][:, :w], in0=G[sl][:, s:s + w],
                                                                                     scalar=cw[:, j, 0:1], in1=c1[x][:, :w],
                                                                                     op0=ALU.mult, op1=ALU.add),
                             reads=[("G", sl, ti), "cw"] + prev, writes=[("c1", x)])
                        S.op("act", lambda e, x=x: e.activation(out=sg[x][:, :w], in_=c1[x][:, :w], func=AF.Silu),
                             reads=[("c1", x)], writes=[("sg", x)])
                        S.op("dve", lambda e, x=x, jj=jj, upp=upp: e.tensor_tensor(out=mT[:, jj, s:s + w], in0=sg[x][:, :w], in1=upp[:, :w],
                                                                                   op=ALU.mult),
                             reads=[("sg", x), ("ps", 5 + gq)], writes=[("m", jj, ti)])
                    for ti, (s, w) in enumerate(tiles):
                        ffn_tile(ti, s, w, j, jj, sl)
                def down_tile(ti, s, w, grp):
                    for dm in range(8):
                        k = cnt["yp"] % 2; cnt["yp"] += 1
                        for jj in range(len(grp)):
                            S.op("pe", lambda e, dm=dm, jj=jj, k=k, grp=grp: e.matmul(ps[k][:, :w], lhsT=wd[:, jj, dm * 128:(dm + 1) * 128],
                                                                                    rhs=mT[:, jj, s:s + w], start=(jj == 0),
                                                                                    stop=(jj == len(grp) - 1)),
                                 reads=["wd", ("m", jj, ti)], writes=[("ps", k)])
                        S.op("dve", lambda e, dm=dm, k=k: e.tensor_tensor(out=hT[:, dm, s:s + w], in0=hT[:, dm, s:s + w],
                                                                          in1=ps[k][:, :w], op=ALU.add),
                             reads=[("h", dm, ti), ("ps", k)], writes=[("h", dm, ti)])
                for ti, (s, w) in enumerate(tiles):
                    down_tile(ti, s, w, grp)
        fin = []
        t0 = 1 if H else 0
        if mode != "final" and mix:
            for c in range(8):
                fin.append(S.dma("sp", lambda e, c=c: e.dma_start(out=hT_o[:, c, :], in_=hT[:, c, H:]),
                                 reads=[("h", c, ti) for ti in range(len(tiles))], key="sth"))
        for ti in range(t0, len(tiles)):
            if mode == "final":
                norm_tile(ti, g_next, "g_next", hT, "h")
            else:
                norm_tile(ti, g_next, "g_next", actT, "act")
        for c in range(8):
            if mode == "final":
                fin.append(S.dma("sp", lambda e, c=c: e.dma_start(out=y_d[:, c, :], in_=hT[:, c, H:]),
                                 reads=[("h", c, ti) for ti in range(len(tiles))], key="sty"))
            else:
                fin.append(S.dma("sp", lambda e, c=c: e.dma_start(out=nT_o[:, c, :], in_=actT[:, c, H:]),
                                 reads=[("act", c, ti) for ti in range(len(tiles))], key="stn"))
        S.emit(final_waits=fin)
    return nc


S_LEN = 16384
NEG = -30000.0


DEBUG = False
STRICT = True


def build_diff(S_LEN=S_LEN):
    NTILE = S_LEN // 512
    NKB = S_LEN // 128
    nc = bass.Bass("TRN2", target_bir_lowering=False)
    dt_in = lambda n, s, d=F32: nc.dram_tensor(n, s, d, kind="ExternalInput").ap()
    nT_d = dt_in("nT", [128, 8, S_LEN], BF16)
    w_d = dt_in("w", [5, 128, 8, 128])
    tab_d = dt_in("tabs", [128, 4, S_LEN])
    lamp_d = dt_in("lamp", [128, 256])
    gsub_d = dt_in("gsub", [128, 1])
    laminit_d = dt_in("laminit", [128, 2])
    ident_d = dt_in("ident", [128, 128])
    tri_d = dt_in("tri", [128, 128])
    oT_d = nc.dram_tensor("oT", [128, S_LEN], BF16, kind="ExternalOutput").ap()

    S = Sched(nc, strict=STRICT)
    with contextlib.ExitStack() as st:
        sb = lambda n, s, d=F32: st.enter_context(nc.sbuf_tensor(n, s, d))
        QT = sb("QT", [128, S_LEN], BF16)
        KT = sb("KT", [128, S_LEN], BF16)
        vT = sb("vT", [128, S_LEN], BF16)
        Vb = sb("Vb", [128, NKB, 128], BF16)
        w = sb("w_sb", [128, 5, 8, 128], BF16)
        ntile = [sb("ntile%d" % i, [128, 8, 512], BF16) for i in range(2)]
        tabs = [sb("tabs%d" % i, [128, 4, 512]) for i in range(2)]
        t1 = [sb("t1_%d" % i, [128, 512]) for i in range(2)]
        t2 = [sb("t2_%d" % i, [128, 512]) for i in range(2)]
        PT = [sb("PT%d" % i, [128, 512], BF16) for i in range(4)]
        ident = sb("ident_sb", [128, 128], BF16)
        tri = sb("tri_sb", [128, 128], BF16)
        ones_bf = sb("ones_bf", [128, 128], BF16)
        lamp = sb("lamp_sb", [128, 256])
        ltmp = sb("ltmp", [128, 64])
        lsum = sb("lsum", [128, 2])
        neglam = sb("neglam", [128, 1])
        gsub = sb("gsub_sb", [128, 1])
        laminit = sb("laminit_sb", [128, 2]); neglam2 = sb("neglam2", [128, 1]); gs2 = sb("gs2", [128, 1])
        e1 = sb("e1", [128, 512]); e2 = sb("e2", [128, 512]); e3 = sb("e3", [128, 512])
        sqb = sb("sqb", [128, 512], BF16)
        obuf = [sb("obuf%d" % i, [128, 512], BF16) for i in range(2)]
        ps = [st.enter_context(nc.psum_tensor("ps%d" % i, [128, 512], F32)) for i in range(8)]
        psbv = [ps[5 + i][:].bitcast(BF16).rearrange("p (a b) -> p a b", b=128) for i in range(2)]

        S.op("pool", lambda e: e.memset(ones_bf[:], 1.0), writes=["ones"])
        for i in range(5):
            S.dma("pool", lambda e, i=i: e.dma_start(out=w[:, i], in_=w_d[i]), writes=[("w", i)], key=("ldw", i))
        S.dma("pool", lambda e: e.dma_start(out=ident[:], in_=ident_d), writes=["ident"], key="ldc1")
        S.dma("pool", lambda e: e.dma_start(out=tri[:], in_=tri_d), writes=["tri"], key="ldc2")
        S.dma("sp", lambda e: e.dma_start(out=lamp[:], in_=lamp_d), writes=["lamp"], key="ldc3")
        S.dma("sp", lambda e: e.dma_start(out=gsub[:], in_=gsub_d), writes=["gsub"], key="ldc4")
        S.dma("sp", lambda e: e.dma_start(out=laminit[:], in_=laminit_d), writes=["laminit"], key="ldc5")
        for i in range(2):
            S.op("dve", lambda e, i=i: e.tensor_tensor(out=ltmp[:], in0=lamp[:, 128 * i:128 * i + 64],
                                                        in1=lamp[:, 128 * i + 64:128 * i + 128], op=ALU.mult),
                 reads=["lamp"], writes=["ltmp"])
            S.op("dve", lambda e, i=i: e.reduce_sum(out=lsum[:, i:i + 1], in_=ltmp[:], axis=mybir.AxisListType.X),
                 reads=["ltmp"], writes=["lsum"])
        S.op("act", lambda e: e.activation(out=lsum[:], in_=lsum[:], func=AF.Exp), reads=["lsum"], writes=["lsum"])
        S.op("dve", lambda e: e.tensor_tensor(out=neglam[:], in0=lsum[:, 1:2], in1=lsum[:, 0:1], op=ALU.subtract),
             reads=["lsum"], writes=["neglam"])
        S.op("dve", lambda e: e.tensor_tensor(out=neglam2[:], in0=neglam[:], in1=laminit[:, 0:1], op=ALU.add),
             reads=["neglam", "laminit"], writes=["neglam2"])
        S.op("dve", lambda e: e.tensor_tensor(out=gs2[:], in0=gsub[:], in1=laminit[:, 1:2], op=ALU.mult),
             reads=["gsub", "laminit"], writes=["gs2"])

        def proj_tile(t):
            sl = t % 2
            c0 = t * 512
            S.dma("sp", lambda e: e.dma_start(out=ntile[sl][:], in_=nT_d[:, :, c0:c0 + 512]), writes=[("nt", sl)], key=("ldn", sl))
            S.dma("act", lambda e: e.dma_start(out=tabs[sl][:], in_=tab_d[:, :, c0:c0 + 512]), writes=[("tab", sl)], key=("ldt", sl))

            def mm(pi, wi):
                for kc in range(8):
                    S.op("pe", lambda e, kc=kc: e.matmul(ps[pi][:], lhsT=w[:, wi, kc, :], rhs=ntile[sl][:, kc, :],
                                                         start=(kc == 0), stop=(kc == 7)),
                         reads=[("w", wi), ("nt", sl)], writes=[("ps", pi)])
            for qi, (dst, dkey, wi, ti) in enumerate(((QT, "QT", 0, 0), (KT, "KT", 2, 2))):
                pa, pb = 2 * qi, 2 * qi + 1
                mm(pa, wi); mm(pb, wi + 1)
                S.op("dve", lambda e, pa=pa, ti=ti, qi=qi: e.tensor_tensor(out=t1[qi][:], in0=ps[pa][:], in1=tabs[sl][:, ti, :], op=ALU.mult),
                     reads=[("ps", pa), ("tab", sl)], writes=[("t1", qi)])
                S.op("dve", lambda e, pb=pb, ti=ti, qi=qi: e.tensor_tensor(out=t2[qi][:], in0=ps[pb][:], in1=tabs[sl][:, ti + 1, :], op=ALU.mult),
                     reads=[("ps", pb), ("tab", sl)], writes=[("t2", qi)])
                S.op("pool", lambda e, dst=dst, qi=qi: e.tensor_tensor(out=dst[:, c0:c0 + 512], in0=t1[qi][:], in1=t2[qi][:], op=ALU.add),
                     reads=[("t1", qi), ("t2", qi)], writes=[(dkey, t)])
            mm(4, 4)
            S.op("act", lambda e: e.activation(out=vT[:, c0:c0 + 512], in_=ps[4][:], func=AF.Copy),
                 reads=[("ps", 4)], writes=[("vT", t)])
        for t in range(NTILE):
            proj_tile(t)

        def vblk(g):
            half = g % 2
            for i in range(4):
                b = 4 * g + i
                S.op("pe", lambda e, b=b, i=i: e.transpose(psbv[half][:, i, :], vT[:, 128 * b:128 * b + 128], ident[:]),
                     reads=[("vT", g), "ident"], writes=[("ps", 5 + half)])
            eng = "dve" if g % 2 == 0 else "act"
            if eng == "dve":
                S.op("dve", lambda e: e.tensor_copy(out=Vb[:, 4 * g:4 * g + 4, :], in_=psbv[half][:, 0:4, :]),
                     reads=[("ps", 5 + half)], writes=[("Vb", g)])
            else:
                S.op("act", lambda e: e.activation(out=Vb[:, 4 * g:4 * g + 4, :], in_=psbv[half][:, 0:4, :], func=AF.Copy),
                     reads=[("ps", 5 + half)], writes=[("Vb", g)])
        for g in range(NTILE):
            vblk(g)

        cnt = {"st": 0, "pt": 0}
        fin = []

        def attn_tile(t):
            nkb = 4 * t + 4
            for kb in range(nkb):
                diag = kb >= 4 * t
                qoff = 128 * (kb - 4 * t) if diag else 0
                N = 512 - qoff
                q0 = 512 * t + qoff
                for s in range(2):
                    r = cnt["st"] % 3; cnt["st"] += 1
                    stp = ps[4 + r]
                    pr = slice(64 * s, 64 * s + 64)
                    S.op("pe", lambda e, stp=stp, pr=pr, N=N, q0=q0, kb=kb, diag=diag: e.matmul(
                        stp[:, :N], lhsT=KT[pr, 128 * kb:128 * kb + 128], rhs=QT[pr, q0:q0 + N], start=True, stop=not diag),
                        reads=[("KT", kb // 4), ("QT", t)], writes=[("ps", 4 + r)])
                    if diag:
                        S.op("pe", lambda e, stp=stp: e.matmul(stp[:, 0:128], lhsT=ident[:], rhs=tri[:], start=False, stop=True),
                             reads=["ident", "tri"], writes=[("ps", 4 + r)])
                    pi = cnt["pt"] % 4; cnt["pt"] += 1
                    S.op("act", lambda e, stp=stp, pi=pi, N=N: e.activation(out=PT[pi][:, :N], in_=stp[:, :N], func=AF.Exp),
                         reads=[("ps", 4 + r)], writes=[("PT", pi)])
                    S.op("pe", lambda e, s=s, pi=pi, N=N, qoff=qoff, kb=kb: e.matmul(
                        ps[s][:, qoff:512], lhsT=Vb[:, kb, :], rhs=PT[pi][:, :N], start=(kb == 0), stop=(kb == nkb - 1)),
                        reads=[("Vb", kb // 4), ("PT", pi)], writes=[("ps", s)])
                    S.op("pe", lambda e, s=s, pi=pi, N=N, qoff=qoff, kb=kb: e.matmul(
                        ps[2 + s][:, qoff:512], lhsT=ones_bf[:], rhs=PT[pi][:, :N], start=(kb == 0), stop=(kb == nkb - 1)),
                        reads=["ones", ("PT", pi)], writes=[("ps", 2 + s)])
            S.op("dve", lambda e: e.reciprocal(out=e1[:], in_=ps[2][:]), reads=[("ps", 2)], writes=["e1"])
            S.op("dve", lambda e: e.tensor_tensor(out=e1[:], in0=ps[0][:], in1=e1[:], op=ALU.mult), reads=[("ps", 0), "e1"], writes=["e1"])
            S.op("dve", lambda e: e.reciprocal(out=e2[:], in_=ps[3][:]), reads=[("ps", 3)], writes=["e2"])
            S.op("dve", lambda e: e.tensor_tensor(out=e2[:], in0=ps[1][:], in1=e2[:], op=ALU.mult), reads=[("ps", 1), "e2"], writes=["e2"])
            S.op("dve", lambda e: e.scalar_tensor_tensor(out=e1[:], in0=e2[:], scalar=neglam2[:, 0:1], in1=e1[:], op0=ALU.mult, op1=ALU.add),
                 reads=["e1", "e2", "neglam2"], writes=["e1"])
            S.op("act", lambda e: e.activation(out=sqb[:], in_=e1[:], func=AF.Square), reads=["e1"], writes=["sqb"])
            r = cnt["st"] % 3; cnt["st"] += 1
            S.op("pe", lambda e: e.matmul(ps[4 + r][:], lhsT=ones_bf[:], rhs=sqb[:], start=True, stop=True),
                 reads=["ones", "sqb"], writes=[("ps", 4 + r)])
            S.op("dve", lambda e: e.tensor_scalar(out=e3[:], in0=ps[4 + r][:], scalar1=1.0 / 128, scalar2=1e-5, op0=ALU.mult, op1=ALU.add),
                 reads=[("ps", 4 + r)], writes=["e3"])
            S.op("act", lambda e: e.activation(out=e3[:], in_=e3[:], func=AF.Sqrt), reads=["e3"], writes=["e3"])
            S.op("dve", lambda e: e.reciprocal(out=e3[:], in_=e3[:]), reads=["e3"], writes=["e3"])
            ob = t % 2
            S.op("dve", lambda e: e.scalar_tensor_tensor(out=obuf[ob][:], in0=e1[:], scalar=gs2[:, 0:1], in1=e3[:], op0=ALU.mult, op1=ALU.mult),
                 reads=["e1", "e3", "gs2"], writes=[("obuf", ob)])
            fin.append(S.dma("sp", lambda e: e.dma_start(out=oT_d[:, 512 * t:512 * t + 512], in_=obuf[ob][:]),
                             reads=[("obuf", ob)], key=("sto", ob)))
        for t in range(NTILE):
            attn_tile(t)
        if DEBUG:
            dq = nc.dram_tensor("dbgQ", [128, S_LEN], BF16, kind="ExternalOutput").ap()
            dk = nc.dram_tensor("dbgK", [128, S_LEN], BF16, kind="ExternalOutput").ap()
            dv = nc.dram_tensor("dbgV", [128, NKB, 128], BF16, kind="ExternalOutput").ap()
            fin.append(S.dma("sp", lambda e: e.dma_start(out=dq, in_=QT[:]), reads=[("QT", t) for t in range(NTILE)], key="dbg"))
            fin.append(S.dma("sp", lambda e: e.dma_start(out=dk, in_=KT[:]), reads=[("KT", t) for t in range(NTILE)], key="dbg"))
            for nm, buf, key in (("dbgE1", e1, "e1"), ("dbgE2", e2, "e2"), ("dbgE3", e3, "e3")):
                dd = nc.dram_tensor(nm, [128, 512], F32, kind="ExternalOutput").ap()
                fin.append(S.dma("sp", lambda e, dd=dd, buf=buf: e.dma_start(out=dd, in_=buf[:]), reads=[key], key="dbg"))
            for i in range(4):
                dd = nc.dram_tensor("dbgP%d" % i, [128, 512], F32, kind="ExternalOutput").ap()
                tb = sb("dbgt%d" % i, [128, 512])
                S.op("dve", lambda e, tb=tb, i=i: e.tensor_copy(out=tb[:], in_=ps[i][:]), reads=[("ps", i)], writes=[("dbgt", i)])
                fin.append(S.dma("sp", lambda e, dd=dd, tb=tb: e.dma_start(out=dd, in_=tb[:]), reads=[("dbgt", i)], key="dbg"))
            fin.append(S.dma("sp", lambda e: e.dma_start(out=dv, in_=Vb[:]), reads=[("Vb", t) for t in range(NTILE)], key="dbg"))
        S.emit(final_waits=fin)
    return nc


NEG = -30000.0
STRICT = True
PATTERNS = (1, 4, 16)


def build_hyb(S_LEN=16384, do_a=True, do_b=True):
    NTILE = S_LEN // 512
    NKB = S_LEN // 128
    NU = S_LEN // 2048
    nc = bass.Bass("TRN2", target_bir_lowering=False)
    dt_in = lambda n, s, d=F32: nc.dram_tensor(n, s, d, kind="ExternalInput").ap()
    nT_d = dt_in("nT", [128, 8, S_LEN], BF16)
    wa_d = dt_in("wa", [5, 128, 8, 64])
    wb_d = dt_in("wb", [128, 8, 193])
    tab_d = dt_in("tabs", [128, 4, S_LEN])
    nbf_d = dt_in("bf", [128, 1])
    ident_d = dt_in("ident", [128, 128])
    tri_d = dt_in("tri", [128, 128])
    mask2_d = dt_in("mask2", [128, 256])
    slt_d = dt_in("slt", [128, 128])
    oT_d = nc.dram_tensor("oT", [128, S_LEN], BF16, kind="ExternalOutput").ap()
    f_scr = nc.dram_tensor("f_scr", [S_LEN], F32).ap()
    c_scr = nc.dram_tensor("c_scr", [6, S_LEN], BF16).ap()

    S = Sched(nc, strict=STRICT)
    with contextlib.ExitStack() as st:
        sb = lambda n, s, d=F32: st.enter_context(nc.sbuf_tensor(n, s, d))
        QT = sb("QT", [128, S_LEN], BF16)
        KT = sb("KT", [128, S_LEN], BF16)
        vT = sb("vT", [128, S_LEN], BF16)
        Vd = sb("Vd", [128, 3, NKB, 64], BF16)
        wa = sb("wa_sb", [128, 5, 8, 64], BF16)
        wb = sb("wb_sb", [128, 8, 193], BF16)
        ntile = [sb("ntile%d" % i, [128, 8, 512], BF16) for i in range(2)]
        tabs = [sb("tabs%d" % i, [128, 4, 512]) for i in range(1)]
        t1 = [sb("t1_%d" % i, [128, 512]) for i in range(2)]
        t2 = [sb("t2_%d" % i, [128, 512]) for i in range(2)]
        PT = [sb("PT%d" % i, [128, 512], BF16) for i in range(4)]
        ident = sb("ident_sb", [128, 128], BF16)
        identf = sb("identf_sb", [128, 128])
        tri = sb("tri_sb", [128, 128], BF16)
        mask2 = sb("mask2_sb", [128, 256], BF16)
        slt = sb("slt_sb", [128, 128])
        ones_bf = sb("ones_bf", [128, 128], BF16)
        ones_f = sb("ones_f", [128, 128])
        nbf = sb("nbf_sb", [128, 1]); fence = sb("fence", [128, 1]); fence2 = sb("fence2", [128, 1])
        e1 = sb("e1", [128, 512]); e2 = sb("e2", [128, 512])
        obuf = [sb("obuf%d" % i, [128, 512], BF16) for i in range(2)]
        fst = [sb("fst%d" % i, [128, 512]) for i in range(2)]
        Fb = sb("Fb", [128, 128]); Fe = sb("Fe", [128, 128]); cs = sb("cs", [128, 128]); offs = sb("offs", [128, 1])
        chi = sb("chi", [128, 128], BF16); chf = sb("chf", [128, 128]); cpk = sb("cpk", [128, 6, 128], BF16)
        ps = [st.enter_context(nc.psum_tensor("ps%d" % i, [128, 512], F32)) for i in range(8)]
        psbv = [ps[5 + i][:].bitcast(BF16).rearrange("p (a b) -> p a b", b=128) for i in range(2)]
        if S_LEN >= 8192:
            accA = vT[0:64, 0:4096].bitcast(F32)
            accL = vT[0:64, 4096:8192].bitcast(F32)
        else:
            accA = sb("accA", [64, 2048])[:]
            accL = sb("accL", [64, 2048])[:]

        S.op("pool", lambda e: e.memset(ones_bf[:], 1.0), writes=["ones"])
        S.op("pool", lambda e: e.memset(ones_f[:], 1.0), writes=["ones_f"])
        for i in range(5):
            S.dma("pool", lambda e, i=i: e.dma_start(out=wa[:, i], in_=wa_d[i]), writes=[("wa", i)], key=("ldw", i))
        S.dma("pool", lambda e: e.dma_start(out=wb[:], in_=wb_d), writes=["wb"], key="ldwb")
        S.dma("pool", lambda e: e.dma_start(out=ident[:], in_=ident_d), writes=["ident"], key="ldc1")
        S.dma("pool", lambda e: e.dma_start(out=tri[:], in_=tri_d), writes=["tri"], key="ldc2")
        S.dma("pool", lambda e: e.dma_start(out=mask2[:], in_=mask2_d), writes=["mask2"], key="ldc3")
        S.dma("sp", lambda e: e.dma_start(out=slt[:], in_=slt_d), writes=["slt"], key="ldc4")
        S.dma("sp", lambda e: e.dma_start(out=fence[:], in_=nbf_d), writes=["bf_raw"], key="ldc5")
        S.op("dve", lambda e: e.tensor_scalar(out=nbf[:], in0=fence[:], scalar1=-1.0, scalar2=None, op0=ALU.mult), reads=["bf_raw"], writes=["nbf"])
        S.dma("sp", lambda e: e.dma_start(out=identf[:], in_=ident_d), writes=["identf"], key="ldc6")
        fin = []
        cnt = {"st": 0, "pt": 0, "tab": 0, "v": 0}

        def load_ntile(t):
            sl = t % 2
            c0 = t * 512
            S.dma("sp", lambda e: e.dma_start(out=ntile[sl][:], in_=nT_d[:, :, c0:c0 + 512]), writes=[("nt", sl)], key=("ldn", sl))
            return sl

        def mm(pi, lhs_fn, M, sl, wkey):
            for kc in range(8):
                S.op("pe", lambda e, kc=kc: e.matmul(ps[pi][0:M, :], lhsT=lhs_fn(kc), rhs=ntile[sl][:, kc, :],
                                                     start=(kc == 0), stop=(kc == 7)),
                     reads=[wkey, ("nt", sl)], writes=[("ps", pi)])

        def a_proj_tile(t):
            sl = load_ntile(t)
            c0 = t * 512
            S.dma("act", lambda e: e.dma_start(out=tabs[0][0:64], in_=tab_d[0:64, :, c0:c0 + 512]), writes=[("tab", 0)], key=("ldt", 0))
            for qi, (dst, dkey, wi, ti) in enumerate(((QT, "QT", 0, 0), (KT, "KT", 2, 2))):
                pa, pb = 2 * qi, 2 * qi + 1
                mm(pa, lambda kc, wi=wi: wa[:, wi, kc, :], 64, sl, ("wa", wi))
                mm(pb, lambda kc, wi=wi: wa[:, wi + 1, kc, :], 64, sl, ("wa", wi + 1))
                S.op("dve", lambda e, pa=pa, ti=ti, qi=qi: e.tensor_tensor(out=t1[qi][0:64], in0=ps[pa][0:64, :], in1=tabs[0][0:64, ti, :], op=ALU.mult),
                     reads=[("ps", pa), ("tab", 0)], writes=[("t1", qi)])
                S.op("dve", lambda e, pb=pb, ti=ti, qi=qi: e.tensor_tensor(out=t2[qi][0:64], in0=ps[pb][0:64, :], in1=tabs[0][0:64, ti + 1, :], op=ALU.mult),
                     reads=[("ps", pb), ("tab", 0)], writes=[("t2", qi)])
                S.op("pool", lambda e, dst=dst, qi=qi: e.tensor_tensor(out=dst[0:64, c0:c0 + 512], in0=t1[qi][0:64], in1=t2[qi][0:64], op=ALU.add),
                     reads=[("t1", qi), ("t2", qi)], writes=[(dkey, t)])
            mm(4, lambda kc: wa[:, 4, kc, :], 64, sl, ("wa", 4))
            S.op("act", lambda e: e.activation(out=vT[0:64, c0:c0 + 512], in_=ps[4][0:64, :], func=AF.Copy),
                 reads=[("ps", 4)], writes=[("vT", t)])

        def a_vblocks():
            for pi, d in enumerate(PATTERNS):
                nj = NKB // d
                for r in range(d):
                    for j0 in range(0, nj, 4):
                        half = cnt["v"] % 2; cnt["v"] += 1
                        nb = min(4, nj - j0)
                        tiles_read = set()
                        for i in range(nb):
                            j = j0 + i
                            tok0 = r + d * 128 * j
                            for tt in range(tok0 // 512, (tok0 + d * 127) // 512 + 1):
                                tiles_read.add(tt)
                        for i in range(nb):
                            j = j0 + i
                            tok0 = r + d * 128 * j
                            S.op("pe", lambda e, i=i, tok0=tok0, d=d, half=half: e.transpose(
                                psbv[half][:, i, 0:64], vT[0:64, tok0:tok0 + d * 127 + 1:d], ident[0:64, 0:64]),
                                reads=[("vT", tt) for tt in tiles_read] + ["ident"], writes=[("ps", 5 + half)])
                        b0 = r * nj + j0
                        if half == 0:
                            S.op("dve", lambda e, pi=pi, b0=b0, nb=nb, half=half: e.tensor_copy(out=Vd[:, pi, b0:b0 + nb, :], in_=psbv[half][:, 0:nb, 0:64]),
                                 reads=[("ps", 5 + half)], writes=[("Vd", pi, b0 // 4)])
                        else:
                            S.op("act", lambda e, pi=pi, b0=b0, nb=nb, half=half: e.activation(out=Vd[:, pi, b0:b0 + nb, :], in_=psbv[half][:, 0:nb, 0:64], func=AF.Copy),
                                 reads=[("ps", 5 + half)], writes=[("Vd", pi, b0 // 4)])

        def a_attn_tile(U):
            first = True
            for pi, d in enumerate(PATTERNS):
                nj = NKB // d
                nbb = 16 // d
                for r in range(d):
                    for bb in range(nbb):
                        j = (16 // d) * U + bb
                        tokq = r + d * 128 * j
                        qtiles = set(range(tokq // 512, (tokq + d * 127) // 512 + 1))
                        rr = cnt["st"] % 3; cnt["st"] += 1
                        stp = ps[5 + rr]
                        has_prev = j > 0
                        N = 256 if has_prev else 128
                        qap = lambda: QT[0:64, tokq:tokq + d * 127 + 1:d]
                        ktiles = set(qtiles)
                        if has_prev:
                            tokp = r + d * 128 * (j - 1)
                            ktiles |= set(range(tokp // 512, (tokp + d * 127) // 512 + 1))
                            S.op("pe", lambda e, stp=stp: e.matmul(stp[:, 0:256], lhsT=ident[:], rhs=mask2[:], start=True, stop=False),
                                 reads=["ident", "mask2"], writes=[("ps", 5 + rr)])
                            S.op("pe", lambda e, stp=stp, tokp=tokp, d=d, tokq=tokq: e.matmul(
                                stp[:, 0:128], lhsT=KT[0:64, tokp:tokp + d * 127 + 1:d], rhs=QT[0:64, tokq:tokq + d * 127 + 1:d], start=False, stop=False),
                                reads=[("KT", x) for x in ktiles] + [("QT", x) for x in qtiles], writes=[("ps", 5 + rr)])
                            S.op("pe", lambda e, stp=stp, d=d, tokq=tokq: e.matmul(
                                stp[:, 128:256], lhsT=KT[0:64, tokq:tokq + d * 127 + 1:d], rhs=QT[0:64, tokq:tokq + d * 127 + 1:d], start=False, stop=True),
                                reads=[("KT", x) for x in ktiles] + [("QT", x) for x in qtiles], writes=[("ps", 5 + rr)])
                        else:
                            S.op("pe", lambda e, stp=stp: e.matmul(stp[:, 0:128], lhsT=ident[:], rhs=mask2[:, 128:256], start=True, stop=False),
                                 reads=["ident", "mask2"], writes=[("ps", 5 + rr)])
                            S.op("pe", lambda e, stp=stp, d=d, tokq=tokq: e.matmul(
                                stp[:, 0:128], lhsT=KT[0:64, tokq:tokq + d * 127 + 1:d], rhs=QT[0:64, tokq:tokq + d * 127 + 1:d], start=False, stop=True),
                                reads=[("KT", x) for x in ktiles] + [("QT", x) for x in qtiles], writes=[("ps", 5 + rr)])
                        pi_ = cnt["pt"] % 4; cnt["pt"] += 1
                        S.op("act", lambda e, stp=stp, pi_=pi_, N=N: e.activation(out=PT[pi_][:, :N], in_=stp[:, :N], func=AF.Exp),
                             reads=[("ps", 5 + rr)], writes=[("PT", pi_)])
                        ab = cnt["st"] % 2
                        blk_cur = r * nj + j
                        if has_prev:
                            for (pp, lhs) in ((ps[ab], None), (ps[2 + ab], ones_bf)):
                                isA = lhs is None
                                S.op("pe", lambda e, pp=pp, isA=isA, pi=pi, blk_cur=blk_cur, pi_=pi_: e.matmul(
                                    pp[0:64, 0:128], lhsT=(Vd[:, pi, blk_cur - 1, :] if isA else ones_bf[:, 0:64]), rhs=PT[pi_][:, 0:128], start=True, stop=False),
                                    reads=[("Vd", pi, (blk_cur - 1) // 4), ("PT", pi_), "ones"], writes=[("ps", ab if isA else 2 + ab)])
                                S.op("pe", lambda e, pp=pp, isA=isA, pi=pi, blk_cur=blk_cur, pi_=pi_: e.matmul(
                                    pp[0:64, 0:128], lhsT=(Vd[:, pi, blk_cur, :] if isA else ones_bf[:, 0:64]), rhs=PT[pi_][:, 128:256], start=False, stop=True),
                                    reads=[("Vd", pi, blk_cur // 4), ("PT", pi_), "ones"], writes=[("ps", ab if isA else 2 + ab)])
                        else:
                            for (pp, lhs) in ((ps[ab], None), (ps[2 + ab], ones_bf)):
                                isA = lhs is None
                                S.op("pe", lambda e, pp=pp, isA=isA, pi=pi, blk_cur=blk_cur, pi_=pi_: e.matmul(
                                    pp[0:64, 0:128], lhsT=(Vd[:, pi, blk_cur, :] if isA else ones_bf[:, 0:64]), rhs=PT[pi_][:, 0:128], start=True, stop=True),
                                    reads=[("Vd", pi, blk_cur // 4), ("PT", pi_), "ones"], writes=[("ps", ab if isA else 2 + ab)])
                        col0 = tokq - 2048 * U
                        cs_ = slice(col0, col0 + d * 127 + 1, d)
                        if pi == 0:
                            S.op("dve", lambda e, ab=ab, cs_=cs_: e.tensor_copy(out=accA[:, cs_], in_=ps[ab][0:64, 0:128]),
                                 reads=[("ps", ab)], writes=["accA"])
                            S.op("dve", lambda e, ab=ab, cs_=cs_: e.tensor_copy(out=accL[:, cs_], in_=ps[2 + ab][0:64, 0:128]),
                                 reads=[("ps", 2 + ab)], writes=["accL"])
                        else:
                            S.op("dve", lambda e, ab=ab, cs_=cs_: e.tensor_tensor(out=accA[:, cs_], in0=accA[:, cs_], in1=ps[ab][0:64, 0:128], op=ALU.add),
                                 reads=[("ps", ab), "accA"], writes=["accA"])
                            S.op("dve", lambda e, ab=ab, cs_=cs_: e.tensor_tensor(out=accL[:, cs_], in0=accL[:, cs_], in1=ps[2 + ab][0:64, 0:128], op=ALU.add),
                                 reads=[("ps", 2 + ab), "accL"], writes=["accL"])
            for c in range(4):
                ob = cnt["pt"] % 2
                S.op("dve", lambda e, c=c: e.reciprocal(out=accL[:, 512 * c:512 * c + 512], in_=accL[:, 512 * c:512 * c + 512]),
                     reads=["accL"], writes=["accL"])
                S.op("dve", lambda e, c=c, ob=ob: e.tensor_tensor(out=obuf[ob][0:64, :], in0=accA[:, 512 * c:512 * c + 512], in1=accL[:, 512 * c:512 * c + 512], op=ALU.mult),
                     reads=["accA", "accL"], writes=[("obuf", ob)])
                tok = 2048 * U + 512 * c
                fin.append(S.dma("sp", lambda e, ob=ob, tok=tok: e.dma_start(out=oT_d[0:64, tok:tok + 512], in_=obuf[ob][0:64, :]),
                                 reads=[("obuf", ob)], key=("sto", ob)))
                cnt["pt"] += 1

        if do_a:
            for t in range(NTILE):
                a_proj_tile(t)
            a_vblocks()
            S.op("dve", lambda e: e.memset(fence2[:], 0.0), reads=[], writes=[("vT", t) for t in range(NTILE)] + ["accA", "accL"])
            for U in range(NU):
                a_attn_tile(U)
            S.op("dve", lambda e: e.memset(fence2[:], 0.0), reads=["accA", "accL"], writes=[("vT", t) for t in range(NTILE)] + ["accA", "accL"])

        def b_proj_tile(t):
            sl = load_ntile(t)
            c0 = t * 512
            mm(0, lambda kc: wb[:, kc, 0:64], 64, sl, "wb")
            S.op("act", lambda e: e.activation(out=QT[0:64, c0:c0 + 512], in_=ps[0][0:64, :], func=AF.Copy, scale=0.125),
                 reads=[("ps", 0)], writes=[("QT", t)])
            mm(1, lambda kc: wb[:, kc, 64:128], 64, sl, "wb")
            S.op("dve", lambda e: e.tensor_copy(out=KT[0:64, c0:c0 + 512], in_=ps[1][0:64, :]),
                 reads=[("ps", 1)], writes=[("KT", t)])
            mm(2, lambda kc: wb[:, kc, 128:193], 65, sl, "wb")
            S.op("act", lambda e: e.activation(out=vT[0:64, c0:c0 + 512], in_=ps[2][0:64, :], func=AF.Copy),
                 reads=[("ps", 2)], writes=[("vT", t)])
            fs = t % 2
            S.op("dve", lambda e: e.tensor_copy(out=fst[fs][64:65, :], in_=ps[2][64:65, :]),
                 reads=[("ps", 2)], writes=[("fst", fs)])
            S.dma("act", lambda e: e.dma_start(out=f_scr.rearrange("(o n) -> o n", o=1)[:, c0:c0 + 512], in_=fst[fs][64:65, :]), reads=[("fst", fs)], writes=["f_scr"], key=("stf", fs))

        def b_gates():
            NB = NKB
            S.dma("sp", lambda e: e.dma_start(out=Fb[0:NB], in_=f_scr.rearrange("(b p) -> b p", p=128)), reads=["f_scr"], writes=["Fb"], key="ldf")
            S.op("act", lambda e: e.activation(out=Fe[0:NB], in_=Fb[0:NB], func=AF.Exp, bias=nbf[0:NB, 0:1], scale=-1.0), reads=["Fb", "nbf"], writes=["Fe"])
            S.op("act", lambda e: e.activation(out=Fe[0:NB], in_=Fe[0:NB], func=AF.Ln, bias=1.0, scale=1.0), reads=["Fe"], writes=["Fe"])
            S.op("dve", lambda e: e.tensor_tensor_scan(out=cs[0:NB], data0=ones_f[0:NB], data1=Fe[0:NB], initial=0.0, op0=ALU.mult, op1=ALU.add),
                 reads=["ones_f", "Fe"], writes=["cs"])
            S.op("pe", lambda e: e.matmul(ps[0][0:NB, 0:1], lhsT=slt[0:NB, 0:NB], rhs=cs[0:NB, 127:128], start=True, stop=True), reads=["slt", "cs"], writes=[("ps", 0)])
            S.op("dve", lambda e: e.tensor_copy(out=offs[0:NB], in_=ps[0][0:NB, 0:1]), reads=[("ps", 0)], writes=["offs"])
            S.op("dve", lambda e: e.tensor_scalar(out=cs[0:NB], in0=cs[0:NB], scalar1=offs[0:NB, 0:1], scalar2=None, op0=ALU.add), reads=["cs", "offs"], writes=["cs"])
            for lvl in range(3):
                S.op("dve", lambda e, lvl=lvl: e.tensor_copy(out=cpk[0:NB, 3 + lvl, :], in_=cs[0:NB]), reads=["cs"], writes=["cpk"])
                S.op("dve", lambda e, lvl=lvl: e.tensor_scalar(out=cpk[0:NB, lvl, :], in0=cpk[0:NB, 3 + lvl, :], scalar1=-1.0, scalar2=None, op0=ALU.mult),
                     reads=["cpk"], writes=["cpk"])
                if lvl < 2:
                    S.op("dve", lambda e, lvl=lvl: e.tensor_copy(out=chf[0:NB], in_=cpk[0:NB, 3 + lvl, :]), reads=["cpk"], writes=["chf"])
                    S.op("dve", lambda e: e.tensor_tensor(out=cs[0:NB], in0=cs[0:NB], in1=chf[0:NB], op=ALU.subtract), reads=["cs", "chf"], writes=["cs"])
            S.dma("sp", lambda e: e.dma_start(out=c_scr.rearrange("j (b p) -> b j p", p=128), in_=cpk[0:NB]), reads=["cpk"], writes=["c_scr"], key="stc")
            allq = [("QT", t) for t in range(NTILE)]; allk = [("KT", t) for t in range(NTILE)]
            S.op("pool", lambda e: e.memset(QT[64:70, :], 1.0), reads=[], writes=["QTaug"])
            S.op("pool", lambda e: e.memset(KT[64:70, :], 1.0), reads=[], writes=["KTaug"])
            S.dma("sp", lambda e: e.dma_start(out=QT[64:67, :], in_=c_scr[0:3, :]), reads=["c_scr", "QTaug"], writes=["QTaug"], key="ldc_q")
            S.dma("sp", lambda e: e.dma_start(out=KT[67:70, :], in_=c_scr[3:6, :]), reads=["c_scr", "KTaug"], writes=["KTaug"], key="ldc_k")

        def b_vblocks():
            for g in range(NTILE):
                half = cnt["v"] % 2; cnt["v"] += 1
                for i in range(4):
                    b = 4 * g + i
                    S.op("pe", lambda e, b=b, i=i, half=half: e.transpose(psbv[half][:, i, 0:64], vT[0:64, 128 * b:128 * b + 128], ident[0:64, 0:64]),
                         reads=[("vT", g), "ident"], writes=[("ps", 5 + half)])
                if half == 0:
                    S.op("dve", lambda e, g=g, half=half: e.tensor_copy(out=Vd[:, 0, 4 * g:4 * g + 4, :], in_=psbv[half][:, 0:4, 0:64]),
                         reads=[("ps", 5 + half)], writes=[("Vd", 0, g)])
                else:
                    S.op("act", lambda e, g=g, half=half: e.activation(out=Vd[:, 0, 4 * g:4 * g + 4, :], in_=psbv[half][:, 0:4, 0:64], func=AF.Copy),
                         reads=[("ps", 5 + half)], writes=[("Vd", 0, g)])

        def b_attn_tile(t):
            nkb = 4 * t + 4
            for kb in range(nkb):
                diag = kb >= 4 * t
                qoff = 128 * (kb - 4 * t) if diag else 0
                N = 512 - qoff
                q0 = 512 * t + qoff
                r = cnt["st"] % 3; cnt["st"] += 1
                stp = ps[5 + r]
                S.op("pe", lambda e, stp=stp, N=N, q0=q0, kb=kb, diag=diag: e.matmul(
                    stp[:, :N], lhsT=KT[0:70, 128 * kb:128 * kb + 128], rhs=QT[0:70, q0:q0 + N], start=True, stop=not diag),
                    reads=[("KT", kb // 4), ("QT", t), "QTaug", "KTaug"], writes=[("ps", 5 + r)])
                if diag:
                    S.op("pe", lambda e, stp=stp: e.matmul(stp[:, 0:128], lhsT=ident[:], rhs=tri[:], start=False, stop=True),
                         reads=["ident", "tri"], writes=[("ps", 5 + r)])
                pi = cnt["pt"] % 4; cnt["pt"] += 1
                S.op("act", lambda e, stp=stp, pi=pi, N=N: e.activation(out=PT[pi][:, :N], in_=stp[:, :N], func=AF.Exp),
                     reads=[("ps", 5 + r)], writes=[("PT", pi)])
                S.op("pe", lambda e, pi=pi, N=N, qoff=qoff, kb=kb: e.matmul(
                    ps[0][0:64, qoff:512], lhsT=Vd[:, 0, kb, :], rhs=PT[pi][:, :N], start=(kb == 0), stop=(kb == nkb - 1)),
                    reads=[("Vd", 0, kb // 4), ("PT", pi)], writes=[("ps", 0)])
                S.op("pe", lambda e, pi=pi, N=N, qoff=qoff, kb=kb: e.matmul(
                    ps[2][0:64, qoff:512], lhsT=ones_bf[:, 0:64], rhs=PT[pi][:, :N], start=(kb == 0), stop=(kb == nkb - 1)),
                    reads=["ones", ("PT", pi)], writes=[("ps", 2)])
            ob = cnt["pt"] % 2; cnt["pt"] += 1
            S.op("dve", lambda e: e.reciprocal(out=e1[0:64], in_=ps[2][0:64, :]), reads=[("ps", 2)], writes=["e1"])
            S.op("dve", lambda e, ob=ob: e.tensor_tensor(out=obuf[ob][0:64, :], in0=ps[0][0:64, :], in1=e1[0:64], op=ALU.mult),
                 reads=[("ps", 0), "e1"], writes=[("obuf", ob)])
            fin.append(S.dma("sp", lambda e, ob=ob: e.dma_start(out=oT_d[64:128, 512 * t:512 * t + 512], in_=obuf[ob][0:64, :]),
                             reads=[("obuf", ob)], key=("sto", ob)))

        if do_b:
            for t in range(NTILE):
                b_proj_tile(t)
            b_gates()
            b_vblocks()
            for t in range(NTILE):
                b_attn_tile(t)
        S.emit(final_waits=fin)
    return nc


import math

D_MODEL = 1024
SEQ = 16384
NCORES = 8
TOK = SEQ // NCORES
FF = 2816
_PROGS = {}


def _prog(name, fn):
    if name not in _PROGS:
        _PROGS[name] = fn()
    return _PROGS[name]


def _fm(a):
    return np.ascontiguousarray(a.T.reshape(8, 128, -1).transpose(1, 0, 2))


def _vec(g):
    return np.ascontiguousarray(g.reshape(8, 128).T)


def _wl(W):
    return np.ascontiguousarray(W.reshape(8, 128, -1).transpose(1, 0, 2))


def _rope_tabs():
    inv = (1.0 / (10000.0 ** (np.arange(0, 64, 2, dtype=np.float32) / 64))).astype(np.float32)
    ang = np.arange(SEQ, dtype=np.float32)[:, None] * inv[None, :]
    ang = np.concatenate([ang, ang], -1)
    cos = np.cos(ang).astype(np.float32); sin = np.sin(ang).astype(np.float32)
    sgn = np.concatenate([-np.ones(32), np.ones(32)]).astype(np.float32)
    t2 = lambda t: np.concatenate([t, t], 1).T
    return np.ascontiguousarray(np.stack([t2(cos) * np.float32(0.125), t2(sin * sgn) * np.float32(0.125), t2(cos), t2(sin * sgn)], 1).astype(np.float32))


def _consts():
    kk = np.arange(128)[:, None]; qq = np.arange(128)[None, :]
    return {
        "ident": np.eye(128, dtype=np.float32),
        "tri": np.where(kk <= qq, 0, NEG).astype(np.float32),
        "mask2": np.concatenate([np.where(kk >= qq, 0, NEG), np.where(kk <= qq, 0, NEG)], 1).astype(np.float32),
        "slt": (kk < qq).astype(np.float32),
    }


def _run(nc, in_maps):
    res = run_bass_kernel_spmd(nc, in_maps, core_ids=list(range(NCORES)))
    return res.results


def _with_halo(per_core, H, dtype):
    out = []
    for c in range(NCORES):
        a = np.zeros((128, 8, H + TOK), dtype)
        a[:, :, H:] = per_core[c]
        if c > 0:
            a[:, :, :H] = per_core[c - 1][:, :, TOK - H:]
        out.append(a)
    return out


def kernel(x, attn_norm, ffn_norm, final_norm, hyb_w_in, hyb_b_f, hyb_w_out,
           diff_w_qkv, diff_lambda, diff_subln, diff_w_out,
           ffn_w_up, ffn_conv_w, ffn_conv_b, ffn_w_down):
    f32 = lambda a: np.asarray(a, dtype=np.float32)
    x = f32(x); attn_norm = f32(attn_norm); ffn_norm = f32(ffn_norm); final_norm = f32(final_norm)
    hyb_w_in = f32(hyb_w_in); hyb_b_f = f32(hyb_b_f); hyb_w_out = f32(hyb_w_out)
    diff_w_qkv = f32(diff_w_qkv); diff_lambda = f32(diff_lambda); diff_subln = f32(diff_subln); diff_w_out = f32(diff_w_out)
    ffn_w_up = f32(ffn_w_up); ffn_conv_w = f32(ffn_conv_w); ffn_conv_b = f32(ffn_conv_b); ffn_w_down = f32(ffn_w_down)
    H = 2
    tabs = _rope_tabs()
    cst = _consts()
    perm = np.concatenate([np.arange(32, 64), np.arange(0, 32)])
    perm2 = np.concatenate([perm, 64 + perm])

    xT = _fm(x[0])
    h_core = [np.ascontiguousarray(xT[:, :, c * TOK:(c + 1) * TOK]) for c in range(NCORES)]
    nc0 = _prog("norm", lambda: build_dense("norm", 0, TOK))
    r = _run(nc0, [{"hT": h_core[c], "g_next": _vec(attn_norm[0])} for c in range(NCORES)])
    nT_full = np.concatenate([r[c]["nT_out"] for c in range(NCORES)], axis=2)
    y = None
    for l in range(4):
        if l % 2 == 0:
            e = l // 2
            W = hyb_w_in[e]
            nch = _prog("hyb", lambda: build_hyb(SEQ))
            maps = []
            for c in range(NCORES):
                cs = slice(64 * c, 64 * c + 64)
                Wqa, Wka, Wva = W[:, 0:512][:, cs], W[:, 512:1024][:, cs], W[:, 1024:1536][:, cs]
                Wqb, Wkb, Wvb = W[:, 1536:2048][:, cs], W[:, 2048:2560][:, cs], W[:, 2560:3072][:, cs]
                Wf = W[:, 3072 + c:3073 + c]
                m = {"nT": nT_full,
                     "wa": np.stack([_wl(Wqa), _wl(Wqa[:, perm]), _wl(Wka), _wl(Wka[:, perm]), _wl(Wva)], 0),
                     "wb": _wl(np.concatenate([Wqb, Wkb, Wvb, Wf], 1)),
                     "tabs": tabs,
                     "bf": np.full((128, 1), hyb_b_f[e, c], np.float32)}
                m.update(cst)
                maps.append(m)
            r = _run(nch, maps)
            rows = np.concatenate([np.arange(64 * k, 64 * k + 64).tolist() + np.arange(512 + 64 * k, 512 + 64 * k + 64).tolist()
                                   for k in range(8)]).astype(np.int64)
            w_out = _wl(hyb_w_out[e][rows])
        else:
            o = l // 2
            W = diff_w_qkv[o]
            ncd = _prog("diff", lambda: build_diff(SEQ))
            lam_init = 0.8 - 0.6 * math.exp(-0.3 * l)
            maps = []
            for c in range(NCORES):
                cs = slice(128 * c, 128 * c + 128)
                Wq, Wk, Wv = W[:, 0:1024][:, cs], W[:, 1024:2048][:, cs], W[:, 2048:3072][:, cs]
                m = {"nT": nT_full,
                     "w": np.stack([_wl(Wq), _wl(Wq[:, perm2]), _wl(Wk), _wl(Wk[:, perm2]), _wl(Wv)], 0),
                     "tabs": tabs,
                     "lamp": np.ascontiguousarray(np.broadcast_to(diff_lambda[o].reshape(1, 256), (128, 256))),
                     "gsub": np.ascontiguousarray(diff_subln[o].reshape(128, 1)),
                     "laminit": np.ascontiguousarray(np.broadcast_to(np.array([[-lam_init, 1.0 - lam_init]], np.float32), (128, 2))),
                     "ident": cst["ident"], "tri": cst["tri"]}
                maps.append(m)
            r = _run(ncd, maps)
            w_out = _wl(diff_w_out[o])
        o_core = [np.stack([r[k]["oT"][:, c * TOK:(c + 1) * TOK] for k in range(NCORES)], axis=1) for c in range(NCORES)]
        oT_h = _with_halo(o_core, H, o_core[0].dtype)
        hT_h = _with_halo(h_core, H, np.float32)
        mode = "final" if l == 3 else "full"
        ncx = _prog(mode, lambda: build_dense(mode, H, TOK))
        g_next = final_norm if l == 3 else attn_norm[l + 1]
        wup = ffn_w_up[l]
        base = {"w_out": w_out, "g_ffn": _vec(ffn_norm[l]), "g_next": _vec(g_next),
                "w_up": np.ascontiguousarray(np.concatenate([wup[:, :FF].reshape(8, 128, 22, 128), wup[:, FF:].reshape(8, 128, 22, 128)], axis=3).transpose(2, 1, 0, 3)),
                "cw": np.ascontiguousarray(ffn_conv_w[l].reshape(3, 22, 128).transpose(2, 1, 0)),
                "cb": np.ascontiguousarray(ffn_conv_b[l].reshape(22, 128).T),
                "w_down": np.ascontiguousarray(ffn_w_down[l].reshape(22, 128, D_MODEL).transpose(1, 0, 2))}
        maps = []
        for c in range(NCORES):
            m = dict(base); m["hT"] = hT_h[c]; m["oT"] = oT_h[c]
            maps.append(m)
        r = _run(ncx, maps)
        if l == 3:
            yT = np.concatenate([r[c]["yT"] for c in range(NCORES)], axis=2)
            y = np.ascontiguousarray(yT.transpose(1, 0, 2).reshape(D_MODEL, SEQ).T)[None].astype(np.float32)
        else:
            h_core = [r[c]["hT_out"] for c in range(NCORES)]
            nT_full = np.concatenate([r[c]["nT_out"] for c in range(NCORES)], axis=2)
    return y
```

```python
import numpy as np, contextlib
import ml_dtypes
import concourse.bass as bass
import concourse.mybir as mybir
from concourse.bass_utils import run_bass_kernel_spmd
F32 = mybir.dt.float32; BF16 = mybir.dt.bfloat16
AF = mybir.ActivationFunctionType; ALU = mybir.AluOpType
NPBF = ml_dtypes.bfloat16


ENGINES = ("pe", "act", "dve", "pool", "sp")
STRICT_SAME_ENGINE = ("act", "dve", "pool")


class Op:
    __slots__ = ("eng", "fn", "deps", "depset", "is_dma", "sem", "val", "src", "idx", "inc", "phase")

    def __init__(self, eng, fn, is_dma):
        self.eng = eng; self.fn = fn; self.deps = []; self.is_dma = is_dma
        self.sem = None; self.val = None; self.src = False; self.idx = None; self.inc = 16; self.depset = set(); self.phase = 0


class Sched:
    def __init__(self, nc, strict=True):
        self.nc = nc
        self.ops = {e: [] for e in ENGINES}
        self.last_w = {}
        self.readers = {}
        self.strict = strict
        self.dma_sems = {}
        self.sem_handles = {}
        self.nsem = 0
        self.phase = 0
        self.open_groups = {}

    def _add(self, op, reads, writes):
        op.phase = self.phase
        if "ALL" not in writes:
            reads = list(reads) + ["ALL"]
        deps = []
        for r in reads:
            w = self.last_w.get(r)
            if w is not None:
                deps.append(w)
        for r in writes:
            w = self.last_w.get(r)
            if w is not None:
                deps.append(w)
            deps.extend(self.readers.get(r, ()))
        for d in deps:
            if d is op:
                continue
            if (not d.is_dma) and d.eng == op.eng and not op.is_dma:
                if not (self.strict and op.eng in STRICT_SAME_ENGINE):
                    continue
            if id(d) not in op.depset:
                op.depset.add(id(d))
                op.deps.append(d)
                d.src = True
        for r in reads:
            self.readers.setdefault(r, []).append(op)
        for r in writes:
            self.last_w[r] = op
            self.readers[r] = []
        op.idx = len(self.ops[op.eng])
        self.ops[op.eng].append(op)
        return op

    def op(self, eng, fn, reads=(), writes=()):
        return self._add(Op(eng, fn, False), reads, writes)

    def dma(self, eng, fn, reads=(), writes=(), key=None, inc=16):
        o = Op(eng, fn, True)
        key = key if key is not None else ("dma", eng)
        cnt = self.dma_sems.setdefault(key, [0])
        cnt[0] += inc
        o.sem = key; o.val = cnt[0]; o.inc = inc
        self.open_groups.setdefault(key, []).append(o)
        return self._add(o, reads, writes)

    def seal(self, key):
        v = self.dma_sems[key][0]
        for o in self.open_groups.get(key, []):
            o.val = v
        self.open_groups[key] = []

    def emit(self, final_waits=()):
        nc = self.nc
        for e in ENGINES:
            c = {}
            for o in self.ops[e]:
                if not o.is_dma and o.src:
                    ph = o.phase if e in ("pe", "act", "dve") else 0
                    c[ph] = c.get(ph, 0) + 1
                    o.sem = ("eng", e, ph); o.val = c[ph]
        keys = set()
        for e in ENGINES:
            for o in self.ops[e]:
                if o.sem is not None and (o.is_dma or o.src):
                    keys.add(o.sem)
        keys = sorted(keys, key=str)
        import contextlib
        with contextlib.ExitStack() as st:
            for i, k in enumerate(keys):
                self.sem_handles[k] = st.enter_context(nc.semaphore("s%d" % i))
            block = st.enter_context(nc.Block())
            engmap = {"pe": block.tensor, "act": block.scalar, "dve": block.vector,
                      "pool": block.gpsimd, "sp": block.sync}
            for e in ENGINES:
                ops = self.ops[e]
                if not ops and not (e == "sp" and final_waits):
                    continue

                def body(eng, ops=ops, e=e):
                    waited = {}
                    for o in ops:
                        need = {}
                        for d in o.deps:
                            if need.get(d.sem, 0) < d.val:
                                need[d.sem] = d.val
                        for k, v in need.items():
                            if waited.get(k, 0) >= v:
                                continue
                            eng.wait_ge(self.sem_handles[k], v)
                            waited[k] = v
                        ins = o.fn(eng)
                        if o.is_dma:
                            ins.then_inc(self.sem_handles[o.sem], o.inc)
                        elif o.src:
                            ins.then_inc(self.sem_handles[o.sem], 1)
                    if e == "sp":
                        need = {}
                        for d in final_waits:
                            if need.get(d.sem, 0) < d.val:
                                need[d.sem] = d.val
                        for k, v in need.items():
                            eng.wait_ge(self.sem_handles[k], v)
                engmap[e](body)


NEG = -30000.0
PATTERNS = (1, 4, 16)
NFF = 22
FF_GROUPS = [list(range(0, 4)), list(range(4, 8)), list(range(8, 12)), list(range(12, 16)), list(range(16, 20)), list(range(20, 22))]
HALO = 8
ARENA_BYTES = 203 * 1024


class Ctx:
    pass


class Arena:
    def __init__(self, t):
        self.t = t
        self.off = 0
        self.peak = 0

    def reset(self):
        self.off = 0

    def __call__(self, name, shape, dtype=F32):
        esz = 4 if dtype in (F32, mybir.dt.int32) else 2
        free = 1
        for s in shape[1:]:
            free *= s
        nbytes = (free * esz + 63) // 64 * 64
        assert self.off + nbytes <= ARENA_BYTES, ("arena overflow", name, self.off, nbytes)
        v = self.t[0:shape[0], self.off // 2:(self.off + free * esz) // 2]
        if dtype != BF16:
            v = v.bitcast(dtype)
        if len(shape) == 3:
            v = v.rearrange("p (a b) -> p a b", b=shape[2])
        elif len(shape) == 4:
            v = v.rearrange("p (a b c) -> p a b c", b=shape[2], c=shape[3])
        self.off += nbytes
        self.peak = max(self.peak, self.off)
        return v


def build_fused(S_LEN=16384, NC=8, layers=(0, 1, 2, 3)):
    TOK = S_LEN // NC
    H = HALO
    T = H + TOK
    NTILE = S_LEN // 512
    NKB = S_LEN // 128
    NU = S_LEN // 2048
    TPC = TOK // 512
    nc = bass.Bass("TRN2", target_bir_lowering=False)
    dt_in = lambda n, s, d=F32: nc.dram_tensor(n, s, d, kind="ExternalInput").ap()
    x_d = dt_in("xT", [128, 8, T])
    gattn_d = dt_in("g_attn", [128, 5, 8])
    gffn_d = dt_in("g_ffn", [128, 4, 8])
    tab_d = dt_in("tabs", [128, 4, S_LEN])
    ident_d = dt_in("ident", [128, 128]); tri_d = dt_in("tri", [128, 128]); mask2_d = dt_in("mask2", [128, 256]); slt_d = dt_in("slt", [128, 128])
    idx_d = nc.dram_tensor("idx", [128, 8], mybir.dt.int32, kind="ExternalInput").ap()
    L = {}
    for l in layers:
        d = {}
        if l % 2 == 0:
            d["wa"] = dt_in("wa%d" % l, [5, 128, 8, 64]); d["wb"] = dt_in("wb%d" % l, [128, 8, 193]); d["bf"] = dt_in("bf%d" % l, [128, 1])
        else:
            d["w"] = dt_in("w%d" % l, [5, 128, 8, 128]); d["lamp"] = dt_in("lamp%d" % l, [128, 256])
            d["gsub"] = dt_in("gsub%d" % l, [128, 1]); d["laminit"] = dt_in("laminit%d" % l, [128, 2])
        d["w_out"] = dt_in("w_out%d" % l, [128, 8, 1024]); d["w_up"] = dt_in("w_up%d" % l, [NFF, 128, 8, 256])
        d["cw"] = dt_in("cw%d" % l, [128, NFF, 3]); d["cb"] = dt_in("cb%d" % l, [128, NFF]); d["w_down"] = dt_in("w_down%d" % l, [128, NFF, 1024])
        L[l] = d
    y_d = nc.dram_tensor("yT", [128, 8, TOK], F32, kind="ExternalOutput").ap()
    nsrc = nc.dram_tensor("nsrc", [1024, TOK], BF16).ap()
    nall = nc.dram_tensor("nall", [NC * 1024, TOK], BF16).ap()
    osrc = nc.dram_tensor("osrc", [NC * 128, T], BF16).ap()
    oall = nc.dram_tensor("oall", [NC * NC * 128, T], BF16).ap()
    hsp = nc.dram_tensor("hsp", [128, 8, T], F32).ap()
    f_scr = nc.dram_tensor("f_scr", [S_LEN], F32).ap()
    c_scr = nc.dram_tensor("c_scr", [6, S_LEN], BF16).ap()
    nall_v = nall.rearrange("(r p k) n -> r p k n", r=NC, p=128)
    groups = [list(range(NC))]

    S = Sched(nc, strict=True)
    with contextlib.ExitStack() as st:
        sbp = lambda n, s, d=F32: st.enter_context(nc.sbuf_tensor(n, s, d))
        ident = sbp("ident_sb", [128, 128], BF16); tri = sbp("tri_sb", [128, 128], BF16); mask2 = sbp("mask2_sb", [128, 256], BF16)
        slt = sbp("slt_sb", [128, 128]); ones_bf = sbp("ones_bf", [128, 128], BF16); ones_f = sbp("ones_f", [128, 128])
        gattn = sbp("gattn_sb", [128, 5, 8]); gffn = sbp("gffn_sb", [128, 4, 8]); idx = sbp("idx_sb", [128, 8], mybir.dt.int32)
        zer = sbp("zer", [128, 8], BF16); fence2 = sbp("fence2", [128, 1]); bar = sbp("bar", [128, 1])
        arena_t = sbp("arena", [128, ARENA_BYTES // 2], BF16)
        sb = Arena(arena_t)
        ps = [st.enter_context(nc.psum_tensor("ps%d" % i, [128, 512], F32)) for i in range(8)]
        psbv = [ps[5 + i][:].bitcast(BF16).rearrange("p (a b) -> p a b", b=128) for i in range(2)]

        def barrier():
            S.phase += 1
            S.op("pool", lambda e: e.memset(bar[:], 0.0), reads=[], writes=["ALL"])
            sb.reset()

        S.op("pool", lambda e: e.memset(ones_bf[:], 1.0), writes=["ones"])
        S.op("pool", lambda e: e.memset(ones_f[:], 1.0), writes=["ones_f"])
        S.op("pool", lambda e: e.memset(zer[:], 0.0), writes=["zer"])
        S.dma("pool", lambda e: e.dma_start(out=ident[:], in_=ident_d), writes=["ident"], key="ldcp")
        S.dma("pool", lambda e: e.dma_start(out=tri[:], in_=tri_d), writes=["tri"], key="ldcp")
        S.dma("pool", lambda e: e.dma_start(out=mask2[:], in_=mask2_d), writes=["mask2"], key="ldcp")
        S.dma("sp", lambda e: e.dma_start(out=slt[:], in_=slt_d), writes=["slt"], key="ldc")
        S.dma("sp", lambda e: e.dma_start(out=gattn[:], in_=gattn_d), writes=["gattn"], key="ldc")
        S.dma("sp", lambda e: e.dma_start(out=gffn[:], in_=gffn_d), writes=["gffn"], key="ldc")
        S.dma("sp", lambda e: e.dma_start(out=idx[:], in_=idx_d), writes=["idx"], key="ldc")
        S.dma("sp", lambda e: e.dma_start(out=osrc[0:128, 0:H], in_=zer[:, 0:H]), reads=["zer"], writes=["osrc_z"], key="ldc")
        S.seal("ldc"); S.seal("ldcp")
        cnt = {"st": 0, "pt": 0, "v": 0, "yp": 0, "sq": 0, "gp": 0, "c1": 0}
        fin = []

        def store_o(rows, src_ap, t, reads, key):
            j = t // TPC
            col = H + (t % TPC) * 512
            r0 = rows.start
            n = rows.stop - rows.start
            ops = [S.dma("sp", lambda e: e.dma_start(out=osrc[j * 128 + r0:j * 128 + r0 + n, col:col + 512], in_=src_ap),
                         reads=reads, writes=[("osrc", t, r0)], key=key)]
            if t % TPC == TPC - 1 and j < NC - 1:
                ops.append(S.dma("sp", lambda e: e.dma_start(out=osrc[(j + 1) * 128 + r0:(j + 1) * 128 + r0 + n, 0:H], in_=src_ap[:, 512 - H:512]),
                                 reads=reads, writes=[("osrc_h", t, r0)], key=key))
            S.seal(key)
            return ops

        def load_ntile(ntile, t):
            sl = t % 2
            r, off = t // TPC, (t % TPC) * 512
            S.dma("sp", lambda e: e.dma_start(out=ntile[sl], in_=nall_v[r, :, :, off:off + 512]), reads=["nall"], writes=[("nt", sl)], key=("ldn", sl))
            return sl

        def allgather_n():
            S.dma("pool", lambda e: e.collective_compute("AllGather", ALU.bypass, replica_groups=groups, ins=[nsrc], outs=[nall]),
                  reads=["nsrc"], writes=["nall"], key="cc", inc=1)

        def allgather_o():
            S.dma("pool", lambda e: e.collective_compute("AllGather", ALU.bypass, replica_groups=groups, ins=[osrc], outs=[oall]),
                  reads=["osrc"], writes=["oall"], key="cc", inc=1)

        tiles = [(0, H)] + [(H + 512 * i, 512) for i in range(TPC)]

        def norm_tile(hT, sqb, rs, ti, g_ap, gkey, dst, dstkey):
            s, w = tiles[ti]
            ssp = ps[2]
            for c in range(8):
                q = cnt["sq"] % 2; cnt["sq"] += 1
                S.op("act", lambda e, c=c, q=q: e.activation(out=sqb[:, q, :w], in_=hT[:, c, s:s + w], func=AF.Square),
                     reads=[("h", c, ti)], writes=[("sq", q)])
                S.op("pe", lambda e, c=c, q=q: e.matmul(ssp[:, :w], lhsT=ones_bf[:], rhs=sqb[:, q, :w], start=(c == 0), stop=(c == 7)),
                     reads=["ones", ("sq", q)], writes=[("ps", 2)])
            S.op("dve", lambda e: e.tensor_scalar(out=rs[:, :w], in0=ssp[:, :w], scalar1=1.0 / 1024, scalar2=1e-6, op0=ALU.mult, op1=ALU.add),
                 reads=[("ps", 2)], writes=["rs"])
            S.op("act", lambda e: e.activation(out=rs[:, :w], in_=rs[:, :w], func=AF.Sqrt), reads=["rs"], writes=["rs"])
            S.op("dve", lambda e: e.reciprocal(out=rs[:, :w], in_=rs[:, :w]), reads=["rs"], writes=["rs"])
            for c in range(8):
                S.op("dve", lambda e, c=c: e.scalar_tensor_tensor(out=dst[:, c, s:s + w], in0=hT[:, c, s:s + w], scalar=g_ap[:, c:c + 1], in1=rs[:, :w],
                                                                  op0=ALU.mult, op1=ALU.mult),
                     reads=[("h", c, ti), gkey, "rs"], writes=[(dstkey, c, ti)])

        def phase_dense(l, mode):
            mix = mode != "norm0"
            hT = sb("hT", [128, 8, T]); actT = sb("actT", [128, 8, T], BF16)
            sqb = sb("sqb", [128, 2, 512], BF16); rs = sb("rs", [128, 512])
            h_src = x_d if l == 0 else hsp
            for c in range(8):
                S.dma("sp", lambda e, c=c: e.dma_start(out=hT[:, c, :], in_=h_src[:, c, :]), reads=[("hsp", c)],
                      writes=[("h", c, ti) for ti in range(len(tiles))], key="ldh")
            S.seal("ldh")
            if mix:
                d = L[l]
                w_out = sb("w_out", [128, 8, 1024], BF16)
                cw = sb("cw", [128, NFF, 3]); cb = sb("cb", [128, NFF])
                GM = max(len(g) for g in FF_GROUPS)
                mT = sb("mT", [128, GM, T], BF16); wd = sb("wd", [128, GM, 1024], BF16)
                wu = [sb("wu%d" % i, [128, 8, 256], BF16) for i in range(2)]
                G = [sb("G%d" % i, [128, T + 2]) for i in range(2)]
                c1 = [sb("c1_%d" % i, [128, 512]) for i in range(2)]
                sg = [sb("sg_%d" % i, [128, 512]) for i in range(2)]
                for r in range(8):
                    S.dma("pool", lambda e, r=r: e.indirect_dma_start(out=actT[:, r, :], out_offset=None, in_=oall,
                                                                      in_offset=bass.IndirectOffsetOnAxis(ap=idx[:, r:r + 1], axis=0)),
                          reads=["oall", "idx"], writes=[("act", r, ti) for ti in range(len(tiles))], key="ldo")
                S.seal("ldo")
                S.dma("sp", lambda e: e.dma_start(out=cw, in_=d["cw"]), writes=["cw"], key="ldcwb")
                S.dma("sp", lambda e: e.dma_start(out=cb, in_=d["cb"]), writes=["cb"], key="ldcwb")
                S.seal("ldcwb")
                for c in range(8):
                    S.dma("pool", lambda e, c=c: e.dma_start(out=w_out[:, c, :], in_=d["w_out"][:, c, :]), writes=[("w_out", c)], key="ldwo")
                S.seal("ldwo")
                for i in range(2):
                    S.op("pool", lambda e, i=i: e.memset(G[i][:, 0:2], 0.0), writes=[("Gpre", i)])

                def p1_tile(ti, s, w):
                    for dm in range(8):
                        k = cnt["yp"] % 2; cnt["yp"] += 1
                        for kc in range(8):
                            S.op("pe", lambda e, dm=dm, kc=kc, k=k: e.matmul(ps[k][:, :w], lhsT=w_out[:, kc, dm * 128:(dm + 1) * 128],
                                                                            rhs=actT[:, kc, s:s + w], start=(kc == 0), stop=(kc == 7)),
                                 reads=[("w_out", kc), ("act", kc, ti)], writes=[("ps", k)])
                        S.op("dve", lambda e, dm=dm, k=k: e.tensor_tensor(out=hT[:, dm, s:s + w], in0=hT[:, dm, s:s + w], in1=ps[k][:, :w], op=ALU.add),
                             reads=[("h", dm, ti), ("ps", k)], writes=[("h", dm, ti)])
                    norm_tile(hT, sqb, rs, ti, gffn[:, l, :], "gffn", actT, "act")
                for ti, (s, w) in enumerate(tiles):
                    p1_tile(ti, s, w)
                jc = 0
                for grp in FF_GROUPS:
                    S.dma("pool", lambda e, grp=grp: e.dma_start(out=wd[:, 0:len(grp), :], in_=d["w_down"][:, grp[0]:grp[0] + len(grp), :]),
                          writes=["wd"], key="ldwd")
                    for jj, j in enumerate(grp):
                        sl = jc % 2; jc += 1
                        S.dma("pool", lambda e, j=j, sl=sl: e.dma_start(out=wu[sl], in_=d["w_up"][j]), writes=[("wu", sl)], key=("ldwu", sl))

                        def ffn_tile(ti, s, w, j, jj, sl):
                            gq = cnt["gp"] % 2; cnt["gp"] += 1
                            gp, upp = ps[3 + gq], ps[5 + gq]
                            for kc in range(8):
                                S.op("pe", lambda e, kc=kc: e.matmul(gp[:, :w], lhsT=wu[sl][:, kc, 0:128], rhs=actT[:, kc, s:s + w], start=(kc == 0), stop=(kc == 7)),
                                     reads=[("wu", sl), ("act", kc, ti)], writes=[("ps", 3 + gq)])
                            for kc in range(8):
                                S.op("pe", lambda e, kc=kc: e.matmul(upp[:, :w], lhsT=wu[sl][:, kc, 128:256], rhs=actT[:, kc, s:s + w], start=(kc == 0), stop=(kc == 7)),
                                     reads=[("wu", sl), ("act", kc, ti)], writes=[("ps", 5 + gq)])
                            S.op("act", lambda e: e.activation(out=G[sl][:, 2 + s:2 + s + w], in_=gp[:, :w], func=AF.Copy),
                                 reads=[("ps", 3 + gq)], writes=[("G", sl, ti)])
                            x = cnt["c1"] % 2; cnt["c1"] += 1
                            prev = [("G", sl, ti - 1)] if ti > 0 else [("Gpre", sl)]
                            S.op("dve", lambda e: e.tensor_scalar(out=c1[x][:, :w], in0=G[sl][:, 2 + s:2 + s + w], scalar1=cw[:, j, 2:3], scalar2=cb[:, j:j + 1],
                                                                  op0=ALU.mult, op1=ALU.add),
                                 reads=[("G", sl, ti), "cw", "cb"], writes=[("c1", x)])
                            S.op("dve", lambda e: e.scalar_tensor_tensor(out=c1[x][:, :w], in0=G[sl][:, 1 + s:1 + s + w], scalar=cw[:, j, 1:2], in1=c1[x][:, :w],
                                                                         op0=ALU.mult, op1=ALU.add),
                                 reads=[("G", sl, ti), "cw"] + prev, writes=[("c1", x)])
                            S.op("dve", lambda e: e.scalar_tensor_tensor(out=c1[x][:, :w], in0=G[sl][:, s:s + w], scalar=cw[:, j, 0:1], in1=c1[x][:, :w],
                                                                         op0=ALU.mult, op1=ALU.add),
                                 reads=[("G", sl, ti), "cw"] + prev, writes=[("c1", x)])
                            S.op("act", lambda e: e.activation(out=sg[x][:, :w], in_=c1[x][:, :w], func=AF.Silu), reads=[("c1", x)], writes=[("sg", x)])
                            S.op("dve", lambda e: e.tensor_tensor(out=mT[:, jj, s:s + w], in0=sg[x][:, :w], in1=upp[:, :w], op=ALU.mult),
                                 reads=[("sg", x), ("ps", 5 + gq)], writes=[("m", jj, ti)])
                        for ti, (s, w) in enumerate(tiles):
                            ffn_tile(ti, s, w, j, jj, sl)

                    def down_tile(ti, s, w, grp):
                        for dm in range(8):
                            k = cnt["yp"] % 2; cnt["yp"] += 1
                            for jj in range(len(grp)):
                                S.op("pe", lambda e, dm=dm, jj=jj, k=k: e.matmul(ps[k][:, :w], lhsT=wd[:, jj, dm * 128:(dm + 1) * 128], rhs=mT[:, jj, s:s + w],
                                                                                start=(jj == 0), stop=(jj == len(grp) - 1)),
                                     reads=["wd", ("m", jj, ti)], writes=[("ps", k)])
                            S.op("dve", lambda e, dm=dm, k=k: e.tensor_tensor(out=hT[:, dm, s:s + w], in0=hT[:, dm, s:s + w], in1=ps[k][:, :w], op=ALU.add),
                                 reads=[("h", dm, ti), ("ps", k)], writes=[("h", dm, ti)])
                    for ti, (s, w) in enumerate(tiles):
                        down_tile(ti, s, w, grp)
            if mode == "final":
                for ti in range(1, len(tiles)):
                    norm_tile(hT, sqb, rs, ti, gattn[:, 4, :], "gattn", hT, "h")
                for c in range(8):
                    fin.append(S.dma("sp", lambda e, c=c: e.dma_start(out=y_d[:, c, :], in_=hT[:, c, H:]),
                                     reads=[("h", c, ti) for ti in range(len(tiles))], key="sty"))
            else:
                if mix:
                    for c in range(8):
                        S.dma("sp", lambda e, c=c: e.dma_start(out=hsp[:, c, :], in_=hT[:, c, :]),
                              reads=[("h", c, ti) for ti in range(len(tiles))], writes=[("hsp", c)], key="sth")
                    S.seal("sth")
                gi = 0 if mode == "norm0" else l + 1
                for ti in range(1, len(tiles)):
                    norm_tile(hT, sqb, rs, ti, gattn[:, gi, :], "gattn", actT, "act")
                S.dma("sp", lambda e: e.dma_start(out=nsrc.rearrange("(p k) n -> p k n", k=8), in_=actT[:, :, H:]),
                      reads=[("act", c, ti) for c in range(8) for ti in range(len(tiles))], writes=["nsrc"], key="stn")

        def phase_diff(l):
            d = L[l]
            QT = sb("QT", [128, S_LEN], BF16); KT = sb("KT", [128, S_LEN], BF16); vT = sb("vT", [128, S_LEN], BF16)
            Vb = sb("Vb", [128, NKB, 128], BF16); w = sb("w_sb", [128, 5, 8, 128], BF16)
            ntile = [sb("ntile%d" % i, [128, 8, 512], BF16) for i in range(2)]
            tabs = [sb("tabs%d" % i, [128, 4, 512]) for i in range(2)]
            t1 = [sb("t1_%d" % i, [128, 512]) for i in range(2)]; t2 = [sb("t2_%d" % i, [128, 512]) for i in range(2)]
            PT = [sb("PT%d" % i, [128, 512], BF16) for i in range(4)]
            lamp = sb("lamp_sb", [128, 256]); ltmp = sb("ltmp", [128, 64]); lsum = sb("lsum", [128, 2]); neglam = sb("neglam", [128, 1])
            gsub = sb("gsub_sb", [128, 1]); laminit = sb("laminit_sb", [128, 2]); neglam2 = sb("neglam2", [128, 1]); gs2 = sb("gs2", [128, 1])
            e1 = sb("e1", [128, 512]); e2 = sb("e2", [128, 512]); e3 = sb("e3", [128, 512]); sqb = sb("sqb", [128, 512], BF16)
            obuf = [sb("obuf%d" % i, [128, 512], BF16) for i in range(2)]
            for i in range(5):
                S.dma("pool", lambda e, i=i: e.dma_start(out=w[:, i], in_=d["w"][i]), writes=[("w", i)], key="ldw")
            S.seal("ldw")
            S.dma("sp", lambda e: e.dma_start(out=lamp, in_=d["lamp"]), writes=["lamp"], key="ldl")
            S.dma("sp", lambda e: e.dma_start(out=gsub, in_=d["gsub"]), writes=["gsub"], key="ldl")
            S.dma("sp", lambda e: e.dma_start(out=laminit, in_=d["laminit"]), writes=["laminit"], key="ldl")
            S.seal("ldl")
            for i in range(2):
                S.op("dve", lambda e, i=i: e.tensor_tensor(out=ltmp, in0=lamp[:, 128 * i:128 * i + 64], in1=lamp[:, 128 * i + 64:128 * i + 128], op=ALU.mult),
                     reads=["lamp"], writes=["ltmp"])
                S.op("dve", lambda e, i=i: e.reduce_sum(out=lsum[:, i:i + 1], in_=ltmp, axis=mybir.AxisListType.X), reads=["ltmp"], writes=["lsum"])
            S.op("act", lambda e: e.activation(out=lsum, in_=lsum, func=AF.Exp), reads=["lsum"], writes=["lsum"])
            S.op("dve", lambda e: e.tensor_tensor(out=neglam, in0=lsum[:, 1:2], in1=lsum[:, 0:1], op=ALU.subtract), reads=["lsum"], writes=["neglam"])
            S.op("dve", lambda e: e.tensor_tensor(out=neglam2, in0=neglam, in1=laminit[:, 0:1], op=ALU.add), reads=["neglam", "laminit"], writes=["neglam2"])
            S.op("dve", lambda e: e.tensor_tensor(out=gs2, in0=gsub, in1=laminit[:, 1:2], op=ALU.mult), reads=["gsub", "laminit"], writes=["gs2"])

            def proj_tile(t):
                sl = load_ntile(ntile, t)
                c0 = t * 512
                S.dma("act", lambda e: e.dma_start(out=tabs[sl], in_=tab_d[:, :, c0:c0 + 512]), writes=[("tab", sl)], key=("ldt", sl))

                def mm(pi, wi):
                    for kc in range(8):
                        S.op("pe", lambda e, kc=kc: e.matmul(ps[pi][:], lhsT=w[:, wi, kc, :], rhs=ntile[sl][:, kc, :], start=(kc == 0), stop=(kc == 7)),
                             reads=[("w", wi), ("nt", sl)], writes=[("ps", pi)])
                for qi, (dst, dkey, wi, ti) in enumerate(((QT, "QT", 0, 0), (KT, "KT", 2, 2))):
                    pa, pb = 2 * qi, 2 * qi + 1
                    mm(pa, wi); mm(pb, wi + 1)
                    S.op("dve", lambda e, pa=pa, ti=ti, qi=qi: e.tensor_tensor(out=t1[qi], in0=ps[pa][:], in1=tabs[sl][:, ti, :], op=ALU.mult),
                         reads=[("ps", pa), ("tab", sl)], writes=[("t1", qi)])
                    S.op("dve", lambda e, pb=pb, ti=ti, qi=qi: e.tensor_tensor(out=t2[qi], in0=ps[pb][:], in1=tabs[sl][:, ti + 1, :], op=ALU.mult),
                         reads=[("ps", pb), ("tab", sl)], writes=[("t2", qi)])
                    S.op("pool", lambda e, dst=dst, qi=qi: e.tensor_tensor(out=dst[:, c0:c0 + 512], in0=t1[qi], in1=t2[qi], op=ALU.add),
                         reads=[("t1", qi), ("t2", qi)], writes=[(dkey, t)])
                mm(4, 4)
                S.op("act", lambda e: e.activation(out=vT[:, c0:c0 + 512], in_=ps[4][:], func=AF.Copy), reads=[("ps", 4)], writes=[("vT", t)])
            for t in range(NTILE):
                proj_tile(t)

            def vblk(g):
                half = g % 2
                for i in range(4):
                    b = 4 * g + i
                    S.op("pe", lambda e, b=b, i=i: e.transpose(psbv[half][:, i, :], vT[:, 128 * b:128 * b + 128], ident[:]),
                         reads=[("vT", g), "ident"], writes=[("ps", 5 + half)])
                if half == 0:
                    S.op("dve", lambda e: e.tensor_copy(out=Vb[:, 4 * g:4 * g + 4, :], in_=psbv[half][:, 0:4, :]), reads=[("ps", 5 + half)], writes=[("Vb", g)])
                else:
                    S.op("act", lambda e: e.activation(out=Vb[:, 4 * g:4 * g + 4, :], in_=psbv[half][:, 0:4, :], func=AF.Copy),
                         reads=[("ps", 5 + half)], writes=[("Vb", g)])
            for g in range(NTILE):
                vblk(g)

            def attn_tile(t):
                nkb = 4 * t + 4
                st = {}

                def stage1(kb, s):
                    diag = kb >= 4 * t
                    qoff = 128 * (kb - 4 * t) if diag else 0
                    N = 512 - qoff
                    q0 = 512 * t + qoff
                    r = cnt["st"] % 3; cnt["st"] += 1
                    stp = ps[4 + r]
                    pr = slice(64 * s, 64 * s + 64)
                    S.op("pe", lambda e: e.matmul(stp[:, :N], lhsT=KT[pr, 128 * kb:128 * kb + 128], rhs=QT[pr, q0:q0 + N], start=True, stop=not diag),
                         reads=[("KT", kb // 4), ("QT", t)], writes=[("ps", 4 + r)])
                    if diag:
                        S.op("pe", lambda e: e.matmul(stp[:, 0:128], lhsT=ident[:], rhs=tri[:], start=False, stop=True),
                             reads=["ident", "tri"], writes=[("ps", 4 + r)])
                    pi = cnt["pt"] % 4; cnt["pt"] += 1
                    S.op("act", lambda e: e.activation(out=PT[pi][:, :N], in_=stp[:, :N], func=AF.Exp), reads=[("ps", 4 + r)], writes=[("PT", pi)])
                    st[(kb, s)] = (pi, N, qoff)

                def stage2(kb, s):
                    pi, N, qoff = st[(kb, s)]
                    S.op("pe", lambda e: e.matmul(ps[s][:, qoff:512], lhsT=Vb[:, kb, :], rhs=PT[pi][:, :N], start=(kb == 0), stop=(kb == nkb - 1)),
                         reads=[("Vb", kb // 4), ("PT", pi)], writes=[("ps", s)])
                    S.op("pe", lambda e: e.matmul(ps[2 + s][:, qoff:512], lhsT=ones_bf[:], rhs=PT[pi][:, :N], start=(kb == 0), stop=(kb == nkb - 1)),
                         reads=["ones", ("PT", pi)], writes=[("ps", 2 + s)])
                units = [(kb, s) for kb in range(nkb) for s in range(2)]
                LA = 2
                for i in range(len(units) + LA):
                    if i < len(units):
                        stage1(*units[i])
                    if i - LA >= 0:
                        stage2(*units[i - LA])
                S.op("dve", lambda e: e.reciprocal(out=e1, in_=ps[2][:]), reads=[("ps", 2)], writes=["e1"])
                S.op("dve", lambda e: e.tensor_tensor(out=e1, in0=ps[0][:], in1=e1, op=ALU.mult), reads=[("ps", 0), "e1"], writes=["e1"])
                S.op("dve", lambda e: e.reciprocal(out=e2, in_=ps[3][:]), reads=[("ps", 3)], writes=["e2"])
                S.op("dve", lambda e: e.tensor_tensor(out=e2, in0=ps[1][:], in1=e2, op=ALU.mult), reads=[("ps", 1), "e2"], writes=["e2"])
                S.op("dve", lambda e: e.scalar_tensor_tensor(out=e1, in0=e2, scalar=neglam2[:, 0:1], in1=e1, op0=ALU.mult, op1=ALU.add),
                     reads=["e1", "e2", "neglam2"], writes=["e1"])
                S.op("act", lambda e: e.activation(out=sqb, in_=e1, func=AF.Square), reads=["e1"], writes=["sqb"])
                r = cnt["st"] % 3; cnt["st"] += 1
                S.op("pe", lambda e: e.matmul(ps[4 + r][:], lhsT=ones_bf[:], rhs=sqb, start=True, stop=True), reads=["ones", "sqb"], writes=[("ps", 4 + r)])
                S.op("dve", lambda e: e.tensor_scalar(out=e3, in0=ps[4 + r][:], scalar1=1.0 / 128, scalar2=1e-5, op0=ALU.mult, op1=ALU.add),
                     reads=[("ps", 4 + r)], writes=["e3"])
                S.op("act", lambda e: e.activation(out=e3, in_=e3, func=AF.Sqrt), reads=["e3"], writes=["e3"])
                S.op("dve", lambda e: e.reciprocal(out=e3, in_=e3), reads=["e3"], writes=["e3"])
                ob = t % 2
                S.op("dve", lambda e: e.scalar_tensor_tensor(out=obuf[ob], in0=e1, scalar=gs2[:, 0:1], in1=e3, op0=ALU.mult, op1=ALU.mult),
                     reads=["e1", "e3", "gs2"], writes=[("obuf", ob)])
                store_o(slice(0, 128), obuf[ob], t, [("obuf", ob)], ("sto", ob))
            for t in range(NTILE):
                attn_tile(t)

        def phase_hyb(l):
            d = L[l]
            QT = sb("QT", [128, S_LEN], BF16); KT = sb("KT", [128, S_LEN], BF16); vT = sb("vT", [128, S_LEN], BF16)
            Vd = sb("Vd", [128, 3, NKB, 64], BF16)
            wa = sb("wa_sb", [128, 5, 8, 64], BF16); wb = sb("wb_sb", [128, 8, 193], BF16)
            ntile = [sb("ntile%d" % i, [128, 8, 512], BF16) for i in range(2)]
            tabs = [sb("tabs0", [128, 4, 512])]
            t1 = [sb("t1_%d" % i, [128, 512]) for i in range(2)]; t2 = [sb("t2_%d" % i, [128, 512]) for i in range(2)]
            PT = [sb("PT%d" % i, [128, 512], BF16) for i in range(4)]
            nbf = sb("nbf", [128, 1]); bfr = sb("bfr", [128, 1])
            e1 = sb("e1", [128, 512])
            obuf = [sb("obuf%d" % i, [128, 512], BF16) for i in range(2)]
            fst = [sb("fst%d" % i, [128, 512]) for i in range(2)]
            Fb = sb("Fb", [128, 128]); Fe = sb("Fe", [128, 128]); cs = sb("cs", [128, 128]); offs = sb("offs", [128, 1])
            chf = sb("chf", [128, 128]); cpk = sb("cpk", [128, 6, 128], BF16)
            if S_LEN >= 8192:
                accA = vT[0:64, 0:4096].bitcast(F32)
                accL = vT[0:64, 4096:8192].bitcast(F32)
            else:
                accA = sb("accA", [64, 2048]); accL = sb("accL", [64, 2048])
            for i in range(5):
                S.dma("pool", lambda e, i=i: e.dma_start(out=wa[:, i], in_=d["wa"][i]), writes=[("wa", i)], key="ldw")
            S.dma("pool", lambda e: e.dma_start(out=wb, in_=d["wb"]), writes=["wb"], key="ldw")
            S.seal("ldw")
            S.dma("sp", lambda e: e.dma_start(out=bfr, in_=d["bf"]), writes=["bf_raw"], key="ldl")
            S.seal("ldl")
            S.op("dve", lambda e: e.tensor_scalar(out=nbf, in0=bfr, scalar1=-1.0, scalar2=None, op0=ALU.mult), reads=["bf_raw"], writes=["nbf"])

            def mm(pi, lhs_fn, M, sl, wkey):
                for kc in range(8):
                    S.op("pe", lambda e, kc=kc: e.matmul(ps[pi][0:M, :], lhsT=lhs_fn(kc), rhs=ntile[sl][:, kc, :], start=(kc == 0), stop=(kc == 7)),
                         reads=[wkey, ("nt", sl)], writes=[("ps", pi)])

            def a_proj_tile(t):
                sl = load_ntile(ntile, t)
                c0 = t * 512
                S.dma("act", lambda e: e.dma_start(out=tabs[0][0:64], in_=tab_d[0:64, :, c0:c0 + 512]), writes=[("tab", 0)], key=("ldt", 0))
                for qi, (dst, dkey, wi, ti) in enumerate(((QT, "QT", 0, 0), (KT, "KT", 2, 2))):
                    pa, pb = 2 * qi, 2 * qi + 1
                    mm(pa, lambda kc, wi=wi: wa[:, wi, kc, :], 64, sl, ("wa", wi))
                    mm(pb, lambda kc, wi=wi: wa[:, wi + 1, kc, :], 64, sl, ("wa", wi + 1))
                    S.op("dve", lambda e, pa=pa, ti=ti, qi=qi: e.tensor_tensor(out=t1[qi][0:64], in0=ps[pa][0:64, :], in1=tabs[0][0:64, ti, :], op=ALU.mult),
                         reads=[("ps", pa), ("tab", 0)], writes=[("t1", qi)])
                    S.op("dve", lambda e, pb=pb, ti=ti, qi=qi: e.tensor_tensor(out=t2[qi][0:64], in0=ps[pb][0:64, :], in1=tabs[0][0:64, ti + 1, :], op=ALU.mult),
                         reads=[("ps", pb), ("tab", 0)], writes=[("t2", qi)])
                    S.op("pool", lambda e, dst=dst, qi=qi: e.tensor_tensor(out=dst[0:64, c0:c0 + 512], in0=t1[qi][0:64], in1=t2[qi][0:64], op=ALU.add),
                         reads=[("t1", qi), ("t2", qi)], writes=[(dkey, t)])
                mm(4, lambda kc: wa[:, 4, kc, :], 64, sl, ("wa", 4))
                S.op("act", lambda e: e.activation(out=vT[0:64, c0:c0 + 512], in_=ps[4][0:64, :], func=AF.Copy), reads=[("ps", 4)], writes=[("vT", t)])

            def a_vblocks():
                for pi, dd in enumerate(PATTERNS):
                    nj = NKB // dd
                    for r in range(dd):
                        for j0 in range(0, nj, 4):
                            half = cnt["v"] % 2; cnt["v"] += 1
                            nb = min(4, nj - j0)
                            tiles_read = set()
                            for i in range(nb):
                                tok0 = r + dd * 128 * (j0 + i)
                                tiles_read |= set(range(tok0 // 512, (tok0 + dd * 127) // 512 + 1))
                            for i in range(nb):
                                tok0 = r + dd * 128 * (j0 + i)
                                S.op("pe", lambda e, i=i, tok0=tok0, dd=dd, half=half: e.transpose(
                                    psbv[half][:, i, 0:64], vT[0:64, tok0:tok0 + dd * 127 + 1:dd], ident[0:64, 0:64]),
                                    reads=[("vT", tt) for tt in tiles_read] + ["ident"], writes=[("ps", 5 + half)])
                            b0 = r * nj + j0
                            if half == 0:
                                S.op("dve", lambda e, pi=pi, b0=b0, nb=nb, half=half: e.tensor_copy(out=Vd[:, pi, b0:b0 + nb, :], in_=psbv[half][:, 0:nb, 0:64]),
                                     reads=[("ps", 5 + half)], writes=[("Vd", pi, b0 // 4)])
                            else:
                                S.op("act", lambda e, pi=pi, b0=b0, nb=nb, half=half: e.activation(out=Vd[:, pi, b0:b0 + nb, :], in_=psbv[half][:, 0:nb, 0:64], func=AF.Copy),
                                     reads=[("ps", 5 + half)], writes=[("Vd", pi, b0 // 4)])

            def a_attn_tile(U):
                st = {}

                def stage1(pi, dd, r, bb):
                    nj = NKB // dd
                    nbb = 16 // dd
                    j = nbb * U + bb
                    tokq = r + dd * 128 * j
                    qtiles = set(range(tokq // 512, (tokq + dd * 127) // 512 + 1))
                    rr = cnt["st"] % 3; cnt["st"] += 1
                    stp = ps[5 + rr]
                    has_prev = j > 0
                    N = 256 if has_prev else 128
                    ktiles = set(qtiles)
                    qsl = slice(tokq, tokq + dd * 127 + 1, dd)
                    if has_prev:
                        tokp = r + dd * 128 * (j - 1)
                        psl = slice(tokp, tokp + dd * 127 + 1, dd)
                        ktiles |= set(range(tokp // 512, (tokp + dd * 127) // 512 + 1))
                        kq_reads = [("KT", x) for x in ktiles] + [("QT", x) for x in qtiles]
                        S.op("pe", lambda e: e.matmul(stp[:, 0:256], lhsT=ident[:], rhs=mask2[:], start=True, stop=False),
                             reads=["ident", "mask2"], writes=[("ps", 5 + rr)])
                        S.op("pe", lambda e: e.matmul(stp[:, 0:128], lhsT=KT[0:64, psl], rhs=QT[0:64, qsl], start=False, stop=False),
                             reads=kq_reads, writes=[("ps", 5 + rr)])
                        S.op("pe", lambda e: e.matmul(stp[:, 128:256], lhsT=KT[0:64, qsl], rhs=QT[0:64, qsl], start=False, stop=True),
                             reads=kq_reads, writes=[("ps", 5 + rr)])
                    else:
                        kq_reads = [("KT", x) for x in ktiles] + [("QT", x) for x in qtiles]
                        S.op("pe", lambda e: e.matmul(stp[:, 0:128], lhsT=ident[:], rhs=mask2[:, 128:256], start=True, stop=False),
                             reads=["ident", "mask2"], writes=[("ps", 5 + rr)])
                        S.op("pe", lambda e: e.matmul(stp[:, 0:128], lhsT=KT[0:64, qsl], rhs=QT[0:64, qsl], start=False, stop=True),
                             reads=kq_reads, writes=[("ps", 5 + rr)])
                    pi_ = cnt["pt"] % 4; cnt["pt"] += 1
                    S.op("act", lambda e: e.activation(out=PT[pi_][:, :N], in_=stp[:, :N], func=AF.Exp), reads=[("ps", 5 + rr)], writes=[("PT", pi_)])
                    st[(pi, r, bb)] = (pi_, has_prev, r * nj + j, tokq)

                def stage2(pi, dd, r, bb):
                    pi_, has_prev, blk_cur, tokq = st[(pi, r, bb)]
                    ab = cnt["v"] % 2; cnt["v"] += 1
                    for isA in (True, False):
                        pp = ps[ab] if isA else ps[2 + ab]
                        pkey = ("ps", ab if isA else 2 + ab)
                        if has_prev:
                            S.op("pe", lambda e, pp=pp, isA=isA: e.matmul(
                                pp[0:64, 0:128], lhsT=(Vd[:, pi, blk_cur - 1, :] if isA else ones_bf[:, 0:64]), rhs=PT[pi_][:, 0:128], start=True, stop=False),
                                reads=[("Vd", pi, (blk_cur - 1) // 4), ("PT", pi_), "ones"], writes=[pkey])
                            S.op("pe", lambda e, pp=pp, isA=isA: e.matmul(
                                pp[0:64, 0:128], lhsT=(Vd[:, pi, blk_cur, :] if isA else ones_bf[:, 0:64]), rhs=PT[pi_][:, 128:256], start=False, stop=True),
                                reads=[("Vd", pi, blk_cur // 4), ("PT", pi_), "ones"], writes=[pkey])
                        else:
                            S.op("pe", lambda e, pp=pp, isA=isA: e.matmul(
                                pp[0:64, 0:128], lhsT=(Vd[:, pi, blk_cur, :] if isA else ones_bf[:, 0:64]), rhs=PT[pi_][:, 0:128], start=True, stop=True),
                                reads=[("Vd", pi, blk_cur // 4), ("PT", pi_), "ones"], writes=[pkey])
                    col0 = tokq - 2048 * U
                    cs_ = slice(col0, col0 + dd * 127 + 1, dd)
                    if pi == 0:
                        S.op("dve", lambda e: e.tensor_copy(out=accA[:, cs_], in_=ps[ab][0:64, 0:128]), reads=[("ps", ab)], writes=["accA"])
                        S.op("dve", lambda e: e.tensor_copy(out=accL[:, cs_], in_=ps[2 + ab][0:64, 0:128]), reads=[("ps", 2 + ab)], writes=["accL"])
                    else:
                        S.op("dve", lambda e: e.tensor_tensor(out=accA[:, cs_], in0=accA[:, cs_], in1=ps[ab][0:64, 0:128], op=ALU.add),
                             reads=[("ps", ab), "accA"], writes=["accA"])
                        S.op("dve", lambda e: e.tensor_tensor(out=accL[:, cs_], in0=accL[:, cs_], in1=ps[2 + ab][0:64, 0:128], op=ALU.add),
                             reads=[("ps", 2 + ab), "accL"], writes=["accL"])
                units = [(pi, dd, r, bb) for pi, dd in enumerate(PATTERNS) for r in range(dd) for bb in range(16 // dd)]
                LA = 0
                for i in range(len(units) + LA):
                    if i < len(units):
                        stage1(*units[i])
                    if i - LA >= 0:
                        stage2(*units[i - LA])
                for c in range(4):
                    ob = cnt["pt"] % 2; cnt["pt"] += 1
                    S.op("dve", lambda e, c=c: e.reciprocal(out=accL[:, 512 * c:512 * c + 512], in_=accL[:, 512 * c:512 * c + 512]), reads=["accL"], writes=["accL"])
                    S.op("dve", lambda e, c=c, ob=ob: e.tensor_tensor(out=obuf[ob][0:64, :], in0=accA[:, 512 * c:512 * c + 512], in1=accL[:, 512 * c:512 * c + 512], op=ALU.mult),
                         reads=["accA", "accL"], writes=[("obuf", ob)])
                    store_o(slice(0, 64), obuf[ob][0:64, :], 4 * U + c, [("obuf", ob)], ("sto", ob))

            for t in range(NTILE):
                a_proj_tile(t)
            a_vblocks()
            S.op("dve", lambda e: e.memset(fence2[:], 0.0), reads=[], writes=[("vT", t) for t in range(NTILE)] + ["accA", "accL"])
            for U in range(NU):
                a_attn_tile(U)
            S.op("dve", lambda e: e.memset(fence2[:], 0.0), reads=["accA", "accL"], writes=[("vT", t) for t in range(NTILE)] + ["accA", "accL"])

            def b_proj_tile(t):
                sl = load_ntile(ntile, t)
                c0 = t * 512
                mm(0, lambda kc: wb[:, kc, 0:64], 64, sl, "wb")
                S.op("act", lambda e: e.activation(out=QT[0:64, c0:c0 + 512], in_=ps[0][0:64, :], func=AF.Copy, scale=0.125), reads=[("ps", 0)], writes=[("QT", t)])
                mm(1, lambda kc: wb[:, kc, 64:128], 64, sl, "wb")
                S.op("dve", lambda e: e.tensor_copy(out=KT[0:64, c0:c0 + 512], in_=ps[1][0:64, :]), reads=[("ps", 1)], writes=[("KT", t)])
                mm(2, lambda kc: wb[:, kc, 128:193], 65, sl, "wb")
                S.op("act", lambda e: e.activation(out=vT[0:64, c0:c0 + 512], in_=ps[2][0:64, :], func=AF.Copy), reads=[("ps", 2)], writes=[("vT", t)])
                fs = t % 2
                S.op("dve", lambda e: e.tensor_copy(out=fst[fs][64:65, :], in_=ps[2][64:65, :]), reads=[("ps", 2)], writes=[("fst", fs)])
                S.dma("act", lambda e: e.dma_start(out=f_scr.rearrange("(o n) -> o n", o=1)[:, c0:c0 + 512], in_=fst[fs][64:65, :]),
                      reads=[("fst", fs)], writes=["f_scr"], key=("stf", fs))

            def b_gates():
                NB = NKB
                S.dma("sp", lambda e: e.dma_start(out=Fb[0:NB], in_=f_scr.rearrange("(b p) -> b p", p=128)), reads=["f_scr"], writes=["Fb"], key="ldf")
                S.op("act", lambda e: e.activation(out=Fe[0:NB], in_=Fb[0:NB], func=AF.Exp, bias=nbf[0:NB, 0:1], scale=-1.0), reads=["Fb", "nbf"], writes=["Fe"])
                S.op("act", lambda e: e.activation(out=Fe[0:NB], in_=Fe[0:NB], func=AF.Ln, bias=1.0, scale=1.0), reads=["Fe"], writes=["Fe"])
                S.op("dve", lambda e: e.tensor_tensor_scan(out=cs[0:NB], data0=ones_f[0:NB], data1=Fe[0:NB], initial=0.0, op0=ALU.mult, op1=ALU.add),
                     reads=["ones_f", "Fe"], writes=["cs"])
                S.op("pe", lambda e: e.matmul(ps[0][0:NB, 0:1], lhsT=slt[0:NB, 0:NB], rhs=cs[0:NB, 127:128], start=True, stop=True), reads=["slt", "cs"], writes=[("ps", 0)])
                S.op("dve", lambda e: e.tensor_copy(out=offs[0:NB], in_=ps[0][0:NB, 0:1]), reads=[("ps", 0)], writes=["offs"])
                S.op("dve", lambda e: e.tensor_scalar(out=cs[0:NB], in0=cs[0:NB], scalar1=offs[0:NB, 0:1], scalar2=None, op0=ALU.add), reads=["cs", "offs"], writes=["cs"])
                for lvl in range(3):
                    S.op("dve", lambda e, lvl=lvl: e.tensor_copy(out=cpk[0:NB, 3 + lvl, :], in_=cs[0:NB]), reads=["cs"], writes=["cpk"])
                    S.op("dve", lambda e, lvl=lvl: e.tensor_scalar(out=cpk[0:NB, lvl, :], in0=cpk[0:NB, 3 + lvl, :], scalar1=-1.0, scalar2=None, op0=ALU.mult),
                         reads=["cpk"], writes=["cpk"])
                    if lvl < 2:
                        S.op("dve", lambda e, lvl=lvl: e.tensor_copy(out=chf[0:NB], in_=cpk[0:NB, 3 + lvl, :]), reads=["cpk"], writes=["chf"])
                        S.op("dve", lambda e: e.tensor_tensor(out=cs[0:NB], in0=cs[0:NB], in1=chf[0:NB], op=ALU.subtract), reads=["cs", "chf"], writes=["cs"])
                S.dma("sp", lambda e: e.dma_start(out=c_scr.rearrange("j (b p) -> b j p", p=128), in_=cpk[0:NB]), reads=["cpk"], writes=["c_scr"], key="stc")
                S.op("pool", lambda e: e.memset(QT[64:70, :], 1.0), reads=[], writes=["QTaug"])
                S.op("pool", lambda e: e.memset(KT[64:70, :], 1.0), reads=[], writes=["KTaug"])
                S.dma("sp", lambda e: e.dma_start(out=QT[64:67, :], in_=c_scr[0:3, :]), reads=["c_scr", "QTaug"], writes=["QTaug"], key="ldc_q")
                S.dma("sp", lambda e: e.dma_start(out=KT[67:70, :], in_=c_scr[3:6, :]), reads=["c_scr", "KTaug"], writes=["KTaug"], key="ldc_k")

            def b_vblocks():
                for g in range(NTILE):
                    half = cnt["v"] % 2; cnt["v"] += 1
                    for i in range(4):
                        b = 4 * g + i
                        S.op("pe", lambda e, b=b, i=i, half=half: e.transpose(psbv[half][:, i, 0:64], vT[0:64, 128 * b:128 * b + 128], ident[0:64, 0:64]),
                             reads=[("vT", g), "ident"], writes=[("ps", 5 + half)])
                    if half == 0:
                        S.op("dve", lambda e, g=g, half=half: e.tensor_copy(out=Vd[:, 0, 4 * g:4 * g + 4, :], in_=psbv[half][:, 0:4, 0:64]),
                             reads=[("ps", 5 + half)], writes=[("Vd", 0, g)])
                    else:
                        S.op("act", lambda e, g=g, half=half: e.activation(out=Vd[:, 0, 4 * g:4 * g + 4, :], in_=psbv[half][:, 0:4, 0:64], func=AF.Copy),
                             reads=[("ps", 5 + half)], writes=[("Vd", 0, g)])

            def b_attn_tile(t):
                nkb = 4 * t + 4
                st = {}

                def stage1(kb):
                    diag = kb >= 4 * t
                    qoff = 128 * (kb - 4 * t) if diag else 0
                    N = 512 - qoff
                    q0 = 512 * t + qoff
                    r = cnt["st"] % 3; cnt["st"] += 1
                    stp = ps[5 + r]
                    S.op("pe", lambda e: e.matmul(stp[:, :N], lhsT=KT[0:70, 128 * kb:128 * kb + 128], rhs=QT[0:70, q0:q0 + N], start=True, stop=not diag),
                         reads=[("KT", kb // 4), ("QT", t), "QTaug", "KTaug"], writes=[("ps", 5 + r)])
                    if diag:
                        S.op("pe", lambda e: e.matmul(stp[:, 0:128], lhsT=ident[:], rhs=tri[:], start=False, stop=True),
                             reads=["ident", "tri"], writes=[("ps", 5 + r)])
                    pi = cnt["pt"] % 4; cnt["pt"] += 1
                    S.op("act", lambda e: e.activation(out=PT[pi][:, :N], in_=stp[:, :N], func=AF.Exp), reads=[("ps", 5 + r)], writes=[("PT", pi)])
                    st[kb] = (pi, N, qoff)

                def stage2(kb):
                    pi, N, qoff = st[kb]
                    S.op("pe", lambda e: e.matmul(ps[0][0:64, qoff:512], lhsT=Vd[:, 0, kb, :], rhs=PT[pi][:, :N], start=(kb == 0), stop=(kb == nkb - 1)),
                         reads=[("Vd", 0, kb // 4), ("PT", pi)], writes=[("ps", 0)])
                    S.op("pe", lambda e: e.matmul(ps[2][0:64, qoff:512], lhsT=ones_bf[:, 0:64], rhs=PT[pi][:, :N], start=(kb == 0), stop=(kb == nkb - 1)),
                         reads=["ones", ("PT", pi)], writes=[("ps", 2)])
                LA = 0
                for i in range(nkb + LA):
                    if i < nkb:
                        stage1(i)
                    if i - LA >= 0:
                        stage2(i - LA)
                ob = cnt["pt"] % 2; cnt["pt"] += 1
                S.op("dve", lambda e: e.reciprocal(out=e1[0:64], in_=ps[2][0:64, :]), reads=[("ps", 2)], writes=["e1"])
                S.op("dve", lambda e, ob=ob: e.tensor_tensor(out=obuf[ob][0:64, :], in0=ps[0][0:64, :], in1=e1[0:64], op=ALU.mult),
                     reads=[("ps", 0), "e1"], writes=[("obuf", ob)])
                store_o(slice(64, 128), obuf[ob][0:64, :], t, [("obuf", ob)], ("sto", ob))

            for t in range(NTILE):
                b_proj_tile(t)
            b_gates()
            b_vblocks()
            for t in range(NTILE):
                b_attn_tile(t)

        first = layers[0]
        phase_dense(first, "norm0")
        barrier()
        allgather_n()
        for l in layers:
            barrier()
            if l % 2 == 0:
                phase_hyb(l)
            else:
                phase_diff(l)
            barrier()
            allgather_o()
            barrier()
            phase_dense(l, "final" if l == layers[-1] else "full")
            if l != layers[-1]:
                barrier()
                allgather_n()
        print("arena peak bytes", sb.peak)
        S.emit(final_waits=fin)
    return nc


import math

D_MODEL = 1024
SEQ = 16384
NCORES = 8
TOK = SEQ // NCORES
FF = 2816
_PROGS = {}


def _fm(a):
    return np.ascontiguousarray(a.T.reshape(8, 128, -1).transpose(1, 0, 2))


def _vec(g):
    return np.ascontiguousarray(g.reshape(8, 128).T)


def _wl(W):
    return np.ascontiguousarray(W.reshape(8, 128, -1).transpose(1, 0, 2))


def _rope_tabs():
    inv = (1.0 / (10000.0 ** (np.arange(0, 64, 2, dtype=np.float32) / 64))).astype(np.float32)
    ang = np.arange(SEQ, dtype=np.float32)[:, None] * inv[None, :]
    ang = np.concatenate([ang, ang], -1)
    cos = np.cos(ang).astype(np.float32); sin = np.sin(ang).astype(np.float32)
    sgn = np.concatenate([-np.ones(32), np.ones(32)]).astype(np.float32)
    t2 = lambda t: np.concatenate([t, t], 1).T
    return np.ascontiguousarray(np.stack([t2(cos) * np.float32(0.125), t2(sin * sgn) * np.float32(0.125), t2(cos), t2(sin * sgn)], 1).astype(np.float32))


def kernel(x, attn_norm, ffn_norm, final_norm, hyb_w_in, hyb_b_f, hyb_w_out,
           diff_w_qkv, diff_lambda, diff_subln, diff_w_out,
           ffn_w_up, ffn_conv_w, ffn_conv_b, ffn_w_down):
    f32 = lambda a: np.asarray(a, dtype=np.float32)
    x = f32(x); attn_norm = f32(attn_norm); ffn_norm = f32(ffn_norm); final_norm = f32(final_norm)
    hyb_w_in = f32(hyb_w_in); hyb_b_f = f32(hyb_b_f); hyb_w_out = f32(hyb_w_out)
    diff_w_qkv = f32(diff_w_qkv); diff_lambda = f32(diff_lambda); diff_subln = f32(diff_subln); diff_w_out = f32(diff_w_out)
    ffn_w_up = f32(ffn_w_up); ffn_conv_w = f32(ffn_conv_w); ffn_conv_b = f32(ffn_conv_b); ffn_w_down = f32(ffn_w_down)
    H = HALO
    kk = np.arange(128)[:, None]; qq = np.arange(128)[None, :]
    perm = np.concatenate([np.arange(32, 64), np.arange(0, 32)])
    perm2 = np.concatenate([perm, 64 + perm])
    shared = {
        "tabs": _rope_tabs(),
        "ident": np.eye(128, dtype=np.float32),
        "tri": np.where(kk <= qq, 0, NEG).astype(np.float32),
        "mask2": np.concatenate([np.where(kk >= qq, 0, NEG), np.where(kk <= qq, 0, NEG)], 1).astype(np.float32),
        "slt": (kk < qq).astype(np.float32),
        "g_attn": np.ascontiguousarray(np.stack([_vec(attn_norm[i]) for i in range(4)] + [_vec(final_norm)], 1)),
        "g_ffn": np.ascontiguousarray(np.stack([_vec(ffn_norm[i]) for i in range(4)], 1)),
    }
    for l in range(4):
        wup = ffn_w_up[l]
        shared["w_up%d" % l] = np.ascontiguousarray(np.concatenate([wup[:, :FF].reshape(8, 128, 22, 128), wup[:, FF:].reshape(8, 128, 22, 128)], axis=3).transpose(2, 1, 0, 3))
        shared["cw%d" % l] = np.ascontiguousarray(ffn_conv_w[l].reshape(3, 22, 128).transpose(2, 1, 0))
        shared["cb%d" % l] = np.ascontiguousarray(ffn_conv_b[l].reshape(22, 128).T)
        shared["w_down%d" % l] = np.ascontiguousarray(ffn_w_down[l].reshape(22, 128, D_MODEL).transpose(1, 0, 2))
        if l % 2 == 0:
            rows = np.concatenate([np.arange(64 * k, 64 * k + 64).tolist() + np.arange(512 + 64 * k, 512 + 64 * k + 64).tolist()
                                   for k in range(8)]).astype(np.int64)
            shared["w_out%d" % l] = _wl(hyb_w_out[l // 2][rows])
        else:
            shared["w_out%d" % l] = _wl(diff_w_out[l // 2])
            lam_init = 0.8 - 0.6 * math.exp(-0.3 * l)
            shared["lamp%d" % l] = np.ascontiguousarray(np.broadcast_to(diff_lambda[l // 2].reshape(1, 256), (128, 256)))
            shared["gsub%d" % l] = np.ascontiguousarray(diff_subln[l // 2].reshape(128, 1))
            shared["laminit%d" % l] = np.ascontiguousarray(np.broadcast_to(np.array([[-lam_init, 1.0 - lam_init]], np.float32), (128, 2)))
    xT = _fm(x[0])
    maps = []
    for c in range(NCORES):
        m = dict(shared)
        xh = np.zeros((128, 8, H + TOK), np.float32)
        xh[:, :, H:] = xT[:, :, c * TOK:(c + 1) * TOK]
        if c > 0:
            xh[:, :, :H] = xT[:, :, c * TOK - H:c * TOK]
        m["xT"] = xh
        m["idx"] = np.ascontiguousarray((np.arange(8)[None, :] * (NCORES * 128) + c * 128 + np.arange(128)[:, None]).astype(np.int32))
        for l in range(4):
            if l % 2 == 0:
                W = hyb_w_in[l // 2]
                cs = slice(64 * c, 64 * c + 64)
                Wqa, Wka, Wva = W[:, 0:512][:, cs], W[:, 512:1024][:, cs], W[:, 1024:1536][:, cs]
                Wqb, Wkb, Wvb = W[:, 1536:2048][:, cs], W[:, 2048:2560][:, cs], W[:, 2560:3072][:, cs]
                Wf = W[:, 3072 + c:3073 + c]
                m["wa%d" % l] = np.stack([_wl(Wqa), _wl(Wqa[:, perm]), _wl(Wka), _wl(Wka[:, perm]), _wl(Wva)], 0)
                m["wb%d" % l] = _wl(np.concatenate([Wqb, Wkb, Wvb, Wf], 1))
                m["bf%d" % l] = np.full((128, 1), hyb_b_f[l // 2, c], np.float32)
            else:
                W = diff_w_qkv[l // 2]
                cs = slice(128 * c, 128 * c + 128)
                Wq, Wk, Wv = W[:, 0:1024][:, cs], W[:, 1024:2048][:, cs], W[:, 2048:3072][:, cs]
                m["w%d" % l] = np.stack([_wl(Wq), _wl(Wq[:, perm2]), _wl(Wk), _wl(Wk[:, perm2]), _wl(Wv)], 0)
        maps.append(m)
    if "fused" not in _PROGS:
        _PROGS["fused"] = build_fused(SEQ, NCORES)
    res = run_bass_kernel_spmd(_PROGS["fused"], maps, core_ids=list(range(NCORES)))
    yT = np.concatenate([res.results[c]["yT"] for c in range(NCORES)], axis=2)
    return np.ascontiguousarray(yT.transpose(1, 0, 2).reshape(D_MODEL, SEQ).T)[None].astype(np.float32)
```
